# Optimizing a Trainium2 kernel written in Bass

```python
import jax, jax.numpy as jnp
from jax import lax
import numpy as np

D_MODEL = 1024
BATCH = 8
SEQ = 4096
DEPTH = 2

CHUNK = 64
EPS = 1e-6
D_MIX = D_MODEL
A_WIDTH = D_MIX // 4
A_HEADS = 4
A_HEAD_DIM = A_WIDTH // A_HEADS
SGU_BLOCK = 128
B_WIDTH = D_MIX // 4
POOL_WINDOWS = (2, 4, 8, 16)
B_GROUPS = len(POOL_WINDOWS)
B_GROUP_DIM = B_WIDTH // B_GROUPS
C_WIDTH = D_MIX - A_WIDTH - B_WIDTH
C_HEADS = 8
C_HEAD_DIM = C_WIDTH // C_HEADS
CONV_WIDTH = 4
LRU_C = 8.0
SPLIT_SIZES = (A_WIDTH, A_WIDTH, A_WIDTH, B_WIDTH, B_WIDTH, C_WIDTH, C_WIDTH)
D_IN = sum(SPLIT_SIZES)
SPLIT_POINTS = tuple(int(v) for v in np.cumsum(SPLIT_SIZES)[:-1])

kernel_name = 'hybrid_sgu_pool_rglru_parallel_heads'


def rmsnorm(x, g):
    xf = x.astype(jnp.float32)
    y = xf * lax.rsqrt(jnp.mean(xf * xf, axis=-1, keepdims=True) + EPS) * g.astype(jnp.float32)
    return y.astype(x.dtype)


def sgu_mixer(u, v, norm_g, w_s, b_s):
    bsz, s, _ = v.shape
    v = rmsnorm(v, norm_g)
    vb = v.reshape(bsz, s // SGU_BLOCK, SGU_BLOCK, A_HEADS, A_HEAD_DIM)
    chunk_id = jnp.arange(SGU_BLOCK) // CHUNK
    mask = (chunk_id[None, :] <= chunk_id[:, None]).astype(w_s.dtype)
    w = w_s * mask[None]
    y = jnp.einsum('hij,bnjhd->bnihd', w, vb) + b_s.T[None, None, :, :, None]
    return u * y.reshape(bsz, s, A_WIDTH)


def pool_mixer(xb, w_pool, b_pool, scale):
    bsz, s, _ = xb.shape
    xf = xb.astype(jnp.float32)
    cs = jnp.concatenate([jnp.zeros((bsz, 1, B_WIDTH), jnp.float32), jnp.cumsum(xf, axis=1)], axis=1)
    t = jnp.arange(s)
    outs = []
    for g, win in enumerate(POOL_WINDOWS):
        sl = slice(g * B_GROUP_DIM, (g + 1) * B_GROUP_DIM)
        c = cs[..., sl]
        upper = c[:, 1:]
        lower = jnp.pad(c[:, :s + 1 - win], ((0, 0), (win - 1, 0), (0, 0)))
        count = jnp.minimum(t + 1, win).astype(jnp.float32)[None, :, None]
        outs.append((upper - lower) / count - xf[..., sl])
    p = jnp.stack(outs, axis=2).astype(xb.dtype)
    y = jnp.einsum('bsgi,gio->bsgo', p, w_pool).reshape(bsz, s, B_WIDTH) + b_pool
    return y * scale


def rglru_mixer(xc, conv_w, conv_b, w_a, b_a, w_x, b_x, lam):
    bsz, s, _ = xc.shape
    xp = jnp.pad(xc, ((0, 0), (CONV_WIDTH - 1, 0), (0, 0)))
    conv = conv_b + sum(xp[:, k:k + s] * conv_w[k] for k in range(CONV_WIDTH))
    xh = conv.reshape(bsz, s, C_HEADS, C_HEAD_DIM)
    r = jax.nn.sigmoid(jnp.einsum('bshi,hio->bsho', xh, w_a).reshape(bsz, s, C_WIDTH) + b_a)
    i = jax.nn.sigmoid(jnp.einsum('bshi,hio->bsho', xh, w_x).reshape(bsz, s, C_WIDTH) + b_x)
    log_a = -LRU_C * r.astype(jnp.float32) * jax.nn.softplus(-lam.astype(jnp.float32))
    a = jnp.exp(log_a)
    mult = jnp.sqrt(-jnp.expm1(2.0 * log_a))
    bterm = mult * (i * conv).astype(jnp.float32)

    def combine(left, right):
        a1, b1 = left
        a2, b2 = right
        return a1 * a2, a2 * b1 + b2

    _, h = lax.associative_scan(combine, (a, bterm), axis=1)
    return h.astype(xc.dtype)


def setup_inputs(seed: int = 0) -> dict:
    key = jax.random.key(seed)
    ks = jax.random.split(key, 20)
    f32 = jnp.float32
    x = jax.random.normal(ks[0], (BATCH, SEQ, D_MODEL), f32)
    norm_g = 1.0 + 0.1 * jax.random.normal(ks[1], (DEPTH, D_MODEL), f32)
    w_in = jax.random.normal(ks[2], (DEPTH, D_MODEL, D_IN), f32) * D_MODEL ** -0.5
    sgu_norm_g = 1.0 + 0.1 * jax.random.normal(ks[3], (DEPTH, A_WIDTH), f32)
    sgu_w = jax.random.normal(ks[4], (DEPTH, A_HEADS, SGU_BLOCK, SGU_BLOCK), f32) * SGU_BLOCK ** -0.5
    sgu_b = 1.0 + 0.1 * jax.random.normal(ks[5], (DEPTH, A_HEADS, SGU_BLOCK), f32)
    pool_w = jax.random.normal(ks[6], (DEPTH, B_GROUPS, B_GROUP_DIM, B_GROUP_DIM), f32) * B_GROUP_DIM ** -0.5
    pool_b = 0.02 * jax.random.normal(ks[7], (DEPTH, B_WIDTH), f32)
    pool_scale = 1.0 + 0.1 * jax.random.normal(ks[8], (DEPTH, B_WIDTH), f32)
    conv_w = jax.random.normal(ks[9], (DEPTH, CONV_WIDTH, C_WIDTH), f32) * CONV_WIDTH ** -0.5
    conv_b = 0.02 * jax.random.normal(ks[10], (DEPTH, C_WIDTH), f32)
    lru_wa = jax.random.normal(ks[11], (DEPTH, C_HEADS, C_HEAD_DIM, C_HEAD_DIM), f32) * C_HEAD_DIM ** -0.5
    lru_ba = 0.02 * jax.random.normal(ks[12], (DEPTH, C_WIDTH), f32)
    lru_wx = jax.random.normal(ks[13], (DEPTH, C_HEADS, C_HEAD_DIM, C_HEAD_DIM), f32) * C_HEAD_DIM ** -0.5
    lru_bx = 0.02 * jax.random.normal(ks[14], (DEPTH, C_WIDTH), f32)
    u = jax.random.uniform(ks[15], (DEPTH, C_WIDTH), f32, 0.9, 0.999)
    a0 = u ** (1.0 / LRU_C)
    lru_lambda = jnp.log(a0) - jnp.log1p(-a0)
    branch_norm_g = 1.0 + 0.1 * jax.random.normal(ks[16], (DEPTH, D_MIX), f32)
    w_out = jax.random.normal(ks[17], (DEPTH, D_MIX, D_MODEL), f32) * D_MIX ** -0.5
    final_g = 1.0 + 0.1 * jax.random.normal(ks[18], (D_MODEL,), f32)
    return {'x': x, 'norm_g': norm_g, 'w_in': w_in, 'sgu_norm_g': sgu_norm_g, 'sgu_w': sgu_w,
            'sgu_b': sgu_b, 'pool_w': pool_w, 'pool_b': pool_b, 'pool_scale': pool_scale,
            'conv_w': conv_w, 'conv_b': conv_b, 'lru_wa': lru_wa, 'lru_ba': lru_ba,
            'lru_wx': lru_wx, 'lru_bx': lru_bx, 'lru_lambda': lru_lambda,
            'branch_norm_g': branch_norm_g, 'w_out': w_out, 'final_g': final_g}


def reference(x, norm_g, w_in, sgu_norm_g, sgu_w, sgu_b, pool_w, pool_b, pool_scale,
              conv_w, conv_b, lru_wa, lru_ba, lru_wx, lru_bx, lru_lambda,
              branch_norm_g, w_out, final_g):
    for l in range(DEPTH):
        h = rmsnorm(x, norm_g[l])
        proj = jnp.einsum('bsd,de->bse', h, w_in[l])
        a_u, a_v, a_z, b_x, b_z, c_x, c_z = jnp.split(proj, SPLIT_POINTS, axis=-1)
        ya = sgu_mixer(a_u, a_v, sgu_norm_g[l], sgu_w[l], sgu_b[l])
        yb = pool_mixer(b_x, pool_w[l], pool_b[l], pool_scale[l])
        yc = rglru_mixer(c_x, conv_w[l], conv_b[l], lru_wa[l], lru_ba[l], lru_wx[l], lru_bx[l], lru_lambda[l])
        g = branch_norm_g[l]
        ya = rmsnorm(ya, g[:A_WIDTH]) * jax.nn.silu(a_z)
        yb = rmsnorm(yb, g[A_WIDTH:A_WIDTH + B_WIDTH]) * jax.nn.silu(b_z)
        yc = rmsnorm(yc, g[A_WIDTH + B_WIDTH:]) * jax.nn.silu(c_z)
        y = jnp.concatenate([ya, yb, yc], axis=-1)
        x = x + jnp.einsum('bse,ed->bsd', y, w_out[l])
    return rmsnorm(x, final_g)
```

```python
import numpy as np
import concourse.bass as bass
import concourse.mybir as mybir
from concourse.bass_utils import run_bass_kernel_spmd

F32 = mybir.dt.float32
BF16 = mybir.dt.bfloat16
AF = mybir.ActivationFunctionType
ALU = mybir.AluOpType

SAME_ENG_SYNC = True
SCHEDULE = True
EMBED_WAIT = True
EMBED_PE = True
ALT_ORDER = True
NHN = 1
Z_BANKS = 1
CX_SL = [(6, 0), (0, 0), (0, 1), (7, 0)]
CONV_SL = [(0, 1), (2, 0), (1, 0), (1, 1)]
BX_SL = [(6, 0), (1, 0)]
AU_SL = [(7, 0), (7, 1)]
Z_SL = [(2, 0), (3, 0), (2, 1), (5, 0), (4, 0), (4, 1), (3, 1), (5, 1)]
POUT_B = [3, 4, 5, 2]
GB_B = [1, 7]
SLACK = 0.0
D = 1024
DIN = 2304
EPS = 1e-6


def _caller_line():
    import sys
    f = sys._getframe(2)
    while f is not None and f.f_code.co_name in ("add", "pe", "act", "dve", "pool", "dma", "<lambda>"):
        f = f.f_back
    return f.f_lineno if f is not None else -1


class Prog:
    ENGS = ("sp", "act", "pool", "dve", "pe")

    def __init__(self, nc):
        self.nc = nc
        self.ops = []
        self.bufs = {}
        self.dma_groups = {}
        self.alias = {}

    def dma_group(self, name, kind="seq"):
        if name not in self.dma_groups:
            self.dma_groups[name] = dict(kind=kind, n=0, sem=None)
        return name

    def add(self, eng, fn, reads=(), writes=(), dma=None, banks=(), cost=None, lat=None, tset=None):
        idx = len(self.ops)
        deps = {}
        if self.alias:
            reads = [r for k in reads for r in self.alias.get(k, (k,))]
            writes = [r for k in writes for r in self.alias.get(k, (k,))]
        for bk in banks:
            b = self.bufs.setdefault(("BANK", bk), {"w": None, "r": []})
            if b["w"] is not None:
                deps[b["w"]] = "bank"
            b["w"] = idx
        for k in reads:
            b = self.bufs.setdefault(k, {"w": None, "r": []})
            if b["w"] is not None:
                deps[b["w"]] = "raw"
        for k in writes:
            b = self.bufs.setdefault(k, {"w": None, "r": []})
            if b["w"] is not None and b["w"] not in deps:
                deps[b["w"]] = "waw"
            for r in b["r"]:
                if r not in deps:
                    deps[r] = "war"
        for k in reads:
            self.bufs[k]["r"].append(idx)
        for k in writes:
            b = self.bufs[k]
            b["w"] = idx
            b["r"] = []
        deps.pop(idx, None)
        if cost is None:
            cost = {"pe": 117, "act": 480, "dve": 560, "pool": 760, "sp": 60}[eng]
        op = dict(eng=eng, fn=fn, deps=deps, dma=dma, signal=False, cnt=None, didx=None, idx=idx,
                  cost=cost, lat=(lat if lat is not None else cost), tset=tset, line=_caller_line())
        if dma is not None:
            g = self.dma_groups[self.dma_group(dma)]
            g["n"] += 1
            op["didx"] = g["n"]
        self.ops.append(op)
        return idx

    def pe(self, fn, reads=(), writes=(), banks=(), **kw):
        return self.add("pe", fn, reads, writes, banks=banks, **kw)

    def act(self, fn, reads=(), writes=(), banks=(), **kw):
        return self.add("act", fn, reads, writes, banks=banks, **kw)

    def dve(self, fn, reads=(), writes=(), banks=(), **kw):
        return self.add("dve", fn, reads, writes, banks=banks, **kw)

    def pool(self, fn, reads=(), writes=(), banks=(), **kw):
        return self.add("pool", fn, reads, writes, banks=banks, **kw)

    def dma(self, eng, group, fn, reads=(), writes=(), **kw):
        kw.setdefault("cost", 60 if eng == "sp" else 1200)
        kw.setdefault("lat", 4000)
        return self.add(eng, fn, reads, writes, dma=group, **kw)

    def schedule(self):
        import heapq
        ops = self.ops
        n = len(ops)
        succ = [[] for _ in range(n)]
        indeg = [0] * n
        for o in ops:
            for d in o["deps"]:
                succ[d].append(o["idx"])
                indeg[o["idx"]] += 1
        prio = [0.0] * n
        for i in range(n - 1, -1, -1):
            m = 0.0
            for s in succ[i]:
                if prio[s] > m:
                    m = prio[s]
            prio[i] = ops[i]["lat"] + m
        fin = [0.0] * n
        rdy = [0.0] * n
        ready = {e: [] for e in self.ENGS}
        for o in ops:
            if indeg[o["idx"]] == 0:
                heapq.heappush(ready[o["eng"]], (-prio[o["idx"]], o["idx"]))
        free = {e: 0.0 for e in self.ENGS}
        cur_set = [None]
        last_on = {}
        rdy_src = {}
        order = []
        done = 0
        while done < n:
            best = None
            allc = []
            for e in self.ENGS:
                h = ready[e]
                if not h:
                    continue
                cands = heapq.nsmallest(6, h)
                for (np_, i) in cands:
                    st = max(free[e], rdy[i])
                    pen = 0.0
                    if e == "act" and ops[i]["tset"] is not None and cur_set[0] is not None and ops[i]["tset"] != cur_set[0]:
                        pen = 1300.0
                    allc.append((st + pen, np_, i, e, st, pen))
            mn = min(c[0] for c in allc)
            near = [c for c in allc if c[0] <= mn + SLACK]
            c = min(near, key=lambda c: (c[1], c[0], c[2]))
            key, e, i, st, pen = c[0], c[3], c[2], c[4], c[5]
            ready[e].remove((-prio[i], i))
            heapq.heapify(ready[e])
            o = ops[i]
            if e == "act" and o["tset"] is not None:
                cur_set[0] = o["tset"]
            st = st + pen
            o["st"] = st
            o["why"] = ("eng", last_on.get(e)) if free[e] >= rdy[i] else ("dep", rdy_src.get(i))
            last_on[e] = i
            free[e] = st + o["cost"]
            fin[i] = st + o["lat"] + 60.0
            order.append((st, i))
            done += 1
            for s in succ[i]:
                if fin[i] > rdy[s]:
                    rdy[s] = fin[i]
                    rdy_src[s] = i
                indeg[s] -= 1
                if indeg[s] == 0:
                    heapq.heappush(ready[ops[s]["eng"]], (-prio[s], s))
        order.sort()
        self.order = [i for _, i in order]
        self.sim_time = max(fin)

    def _need_sync(self, x, y, kind):
        if y["dma"] is not None or x["dma"] is not None:
            return True
        if x["eng"] != y["eng"]:
            return True
        if x["eng"] == "pe" or kind == "bank":
            return False
        return SAME_ENG_SYNC

    def emit(self, final_wait_groups=()):
        nc = self.nc
        ops = self.ops
        order = getattr(self, "order", None) or list(range(len(ops)))
        seq = [ops[i] for i in order]
        for x in ops:
            for yi, kind in x["deps"].items():
                y = ops[yi]
                if y["dma"] is None and self._need_sync(x, y, kind):
                    y["signal"] = True
        cnt = {e: 0 for e in self.ENGS}
        gcnt = {g: 0 for g in self.dma_groups}
        for o in seq:
            if o["dma"] is not None:
                gcnt[o["dma"]] += 1
                o["didx"] = gcnt[o["dma"]]
            if o["dma"] is None and o["signal"]:
                cnt[o["eng"]] += 1
                o["cnt"] = cnt[o["eng"]]
        esem = {e: nc.alloc_semaphore("sem_" + e) for e in self.ENGS}
        for name, g in self.dma_groups.items():
            g["sem"] = nc.alloc_semaphore("dsem_" + name)
        self.stats = dict(cnt=dict(cnt), nops={e: 0 for e in self.ENGS},
                          nwaits={e: 0 for e in self.ENGS})

        def wait_target(y):
            if y["dma"] is not None:
                g = self.dma_groups[y["dma"]]
                if g["kind"] == "all":
                    return g["sem"], 16 * g["n"]
                return g["sem"], 16 * y["didx"]
            return esem[y["eng"]], y["cnt"]

        with nc.Block() as block:
            def run_engine(ename, e):
                waited = {}
                for x in seq:
                    if x["eng"] != ename:
                        continue
                    need = {}
                    for yi, kind in x["deps"].items():
                        y = ops[yi]
                        if not self._need_sync(x, y, kind):
                            continue
                        sem, val = wait_target(y)
                        key = id(sem)
                        if val > need.get(key, (None, 0))[1]:
                            need[key] = (sem, val)
                    todo = [(key, sem, val) for key, (sem, val) in need.items() if waited.get(key, 0) < val]
                    emb = None
                    if EMBED_WAIT and todo and x["dma"] is None and (ename != "pe" or EMBED_PE):
                        emb = todo.pop()
                    for key, sem, val in todo:
                        e.wait_ge(sem, val)
                        waited[key] = val
                        self.stats["nwaits"][ename] += 1
                    inst = x["fn"](e)
                    if emb is not None:
                        inst._wait_ge(emb[1], emb[2])
                        waited[emb[0]] = emb[2]
                    self.stats["nops"][ename] += 1
                    if x["dma"] is not None:
                        inst.then_inc(self.dma_groups[x["dma"]]["sem"], 16)
                    elif x["signal"]:
                        inst.then_inc(esem[ename], 1)
                if ename == "sp":
                    for gname in final_wait_groups:
                        g = self.dma_groups[gname]
                        e.wait_ge(g["sem"], 16 * g["n"])

            @block.sync
            def _(e):
                run_engine("sp", e)

            @block.scalar
            def _(e):
                run_engine("act", e)

            @block.gpsimd
            def _(e):
                run_engine("pool", e)

            @block.vector
            def _(e):
                run_engine("dve", e)

            @block.tensor
            def _(e):
                run_engine("pe", e)


PARAM_SHAPES = {
    "norm_g": [2, 1024], "w_in": [2, 1024, 2304], "sgu_norm_g": [2, 256],
    "sgu_w": [2, 4, 128, 128], "sgu_b": [2, 4, 128], "pool_w": [2, 4, 64, 64],
    "pool_b": [2, 256], "pool_scale": [2, 256], "conv_w": [2, 4, 512],
    "conv_b": [2, 512], "lru_wa": [2, 8, 64, 64], "lru_ba": [2, 512],
    "lru_wx": [2, 8, 64, 64], "lru_bx": [2, 512], "lru_lambda": [2, 512],
    "branch_norm_g": [2, 1024], "w_out": [2, 1024, 1024], "final_g": [1024],
}

C_AU, C_AV, C_AZ, C_BX, C_BZ, C_CX, C_CZ = 0, 256, 512, 768, 1024, 1280, 1792


def build(S, T=256, taps=None, NL=2, do_final=True):
    NS = T // 128
    NCH = S // T
    nc = bass.Bass("TRN2", target_bir_lowering=False)
    P = Prog(nc)
    x_d = nc.dram_tensor("x", [S, D], F32, kind="ExternalInput")
    dr = {k: nc.dram_tensor(k, shp, F32, kind="ExternalInput") for k, shp in PARAM_SHAPES.items()}
    out_d = nc.dram_tensor("out", [S, D], F32, kind="ExternalOutput")
    tap_d = {}

    def sb(name, shape, dt=F32):
        return nc.alloc_sbuf_tensor(name, shape, dt)

    WIN = sb("WIN", [128, 2, 8, DIN], BF16)
    WOUT = sb("WOUT", [128, 2, 8, D], BF16)
    NXS = 3 if ALT_ORDER else 2
    XS = sb("XS", [128, NXS, NS, D])
    DIAG = sb("DIAG", [128, 2, 4, 4, 128], BF16)
    GF = sb("GF", [128, D])
    WT = sb("WT", [128, 8, 128], BF16)
    SB_ = sb("SBIAS", [128, 2, 2, 128])
    PW = sb("PW", [128, 2, 2, 128], BF16)
    WA = sb("WA", [128, 2, 4, 128], BF16)
    WX = sb("WX", [128, 2, 4, 128], BF16)
    PC = sb("PC", [128, 128])
    CA = sb("CA", [128, 2, 8])
    TMPS = sb("TMPS", [128, 8, 8])
    NEGH = sb("NEGH", [128, 8])
    IDB = sb("IDB", [128, 128], BF16)
    ONESB = sb("ONESB", [128, 128], BF16)
    INVC = sb("INVC", [128, 2, 16])
    HN = sb("HN", [128, NHN, D], BF16)
    HT = sb("HT", [128, 8, T], BF16)
    SS = sb("SS", [128, 8])
    RS = sb("RS", [128, 8])
    U = sb("U", [128, 2, T])
    VN = sb("VN", [128, NS, 256], BF16)
    BXW = sb("BXW", [128, 2, 2, 16 + T])
    W_ = 16 + T
    ARENA = sb("ARENA", [128, 6 * W_])
    S2 = ARENA[:, 0:2 * W_].rearrange("p (k w) -> p k w", k=2)
    S4 = ARENA[:, 2 * W_:4 * W_].rearrange("p (k w) -> p k w", k=2)
    S8 = ARENA[:, 4 * W_:5 * W_]
    S16 = ARENA[:, 5 * W_:6 * W_]
    AA4 = ARENA[:, 0:4 * T].rearrange("p (k t) -> p k t", k=4)
    TI4 = ARENA[:, 4 * T:6 * T].bitcast(BF16).rearrange("p (k t) -> p k t", k=4)
    K3B = []
    K3C = []
    _bounds = sorted(set([0, 2 * W_, 4 * W_, 5 * W_, 6 * W_] + [k * T for k in range(5)] + [4 * T + k * (T // 2) for k in range(5)]))
    _regs = list(zip(_bounds[:-1], _bounds[1:]))

    def _cover(lo, hi):
        return [("AR", a) for (a, b) in _regs if a < hi and b > lo]
    P.alias["S2"] = _cover(0, 2 * W_)
    P.alias["S4"] = _cover(2 * W_, 4 * W_)
    P.alias["S8"] = _cover(4 * W_, 5 * W_)
    P.alias["S16"] = _cover(5 * W_, 6 * W_)
    for k in range(4):
        P.alias[("AA4", k)] = _cover(k * T, (k + 1) * T)
        P.alias[("TI4", k)] = _cover(4 * T + k * (T // 2), 4 * T + (k + 1) * (T // 2))
    PB = sb("PB", [128, 2, T], BF16)
    YB = sb("YB", [128, 2, T])
    CXW = sb("CXW", [128, 2, 4, 4 + T], BF16)
    CONV = sb("CONV", [128, 4, T])
    CONVB = sb("CONVB", [128, 4, T], BF16)
    M4 = [sb("M4_%d" % k, [128, T]) for k in range(4)]
    BT2 = sb("BT2", [128, 2, T])
    RT2 = sb("RT2", [128, 2, T])
    PR = RT2[:, 0, 0:128]
    ONESF = RT2[:, 0, 128:256]
    IDF = RT2[:, 1, 0:128]
    T1 = BT2
    for hp_ in range(2):
        P.alias[("T1", hp_)] = [("BT2", hp_)]
    YC = sb("YC", [128, 4, T])
    HST = sb("HST", [128, 2, 4])
    SQ = PB
    for q_ in range(2):
        P.alias[("SQ", q_)] = [("PB", q_)]
    RB = sb("RB", [128, 3, T])
    SZ8 = sb("SZ8", [128, 8, T], BF16)
    YT = sb("YT", [128, 8, T], BF16)
    SW = HN[:, 0, :].rearrange("p (q j) -> p q j", q=8)

    ps_tr = [nc.alloc_psum_tensor("ps_tr%d" % i, [128, 1024], BF16) for i in range(2)]
    ps_mm = [nc.alloc_psum_tensor("ps_mm%d" % i, [128, 512], F32) for i in range(2)]
    ps_av = nc.alloc_psum_tensor("ps_av", [128, NS, 256], F32)
    ps_sg = nc.alloc_psum_tensor("ps_sg", [128, 2, T], F32)
    ps_sm = nc.alloc_psum_tensor("ps_sm", [128, 2, T], F32)
    ps_o = nc.alloc_psum_tensor("ps_o", [128, 512], F32)
    B_TR, B_MM, B_AV, B_SG, B_SM, B_O = (0, 1), (2, 3), 4, 5, 6, 7
    PFULL = [(ps_mm[0][:, :], 2), (ps_mm[1][:, :], 3),
             (ps_tr[0][:, :].bitcast(F32), 0), (ps_tr[1][:, :].bitcast(F32), 1)]
    POUT = [(ps_mm[1][:, :], 3), (ps_av[:, :, :].rearrange("p s e -> p (s e)"), 4),
            (ps_sg[:, :, :].rearrange("p h t -> p (h t)"), 5), (ps_mm[0][:, :], 2)]
    GB = [(ps_sm, 6), (ps_o[:, :].rearrange("p (h t) -> p h t", h=2), 7)]
    PSLOT = [(PFULL[i % 4][0][:, (i // 4) * T:(i // 4 + 1) * T], PFULL[i % 4][1]) for i in range(8)]
    _zfull = [(ps_mm[0][:, :], 2), (ps_mm[1][:, :], 3),
              (ps_av[:, :, :].rearrange("p s e -> p (s e)"), 4), (ps_sg[:, :, :].rearrange("p h t -> p (h t)"), 5)]
    _bv = {0: ps_tr[0][:, :].bitcast(F32), 1: ps_tr[1][:, :].bitcast(F32), 2: ps_mm[0][:, :], 3: ps_mm[1][:, :],
           4: ps_av[:, :, :].rearrange("p s e -> p (s e)"), 5: ps_sg[:, :, :].rearrange("p h t -> p (h t)"),
           6: ps_sm[:, :, :].rearrange("p h t -> p (h t)"), 7: ps_o[:, :]}

    def _half(bh):
        return (_bv[bh[0]][:, bh[1] * T:(bh[1] + 1) * T], bh[0])
    CXSLOT = [_half(x) for x in CX_SL]
    CONVSLOT = [_half(x) for x in CONV_SL]
    BXSLOT = [_half(x) for x in BX_SL]
    AUSLOT = [_half(x) for x in AU_SL]
    POUT = [(_bv[b_], b_) for b_ in POUT_B]
    GB = [(_bv[b_].rearrange("p (h t) -> p h t", h=2), b_) for b_ in GB_B]
    ZSLOT = [_half(x) for x in Z_SL] if True else [(_zfull[i % 4][0][:, (i // 4) * T:(i // 4 + 1) * T], _zfull[i % 4][1]) for i in range(8)]

    def tap(name, src_ap, shape, reads):
        if taps is None or name not in taps:
            return
        t = nc.dram_tensor("tap_" + name, shape, F32, kind="ExternalOutput")
        tap_d[name] = t
        P.dma_group("tap", "all")
        P.dma("sp", "tap", lambda e: e.dma_start(out=t.ap(), in_=src_ap), reads=reads)


    nsetup = [0]

    def setup_dma(out_ap, in_ap, key, eng="sp"):
        g = "su%d" % nsetup[0]
        nsetup[0] += 1
        P.dma_group(g, "seq")
        P.dma(eng, g, lambda e: e.dma_start(out=out_ap, in_=in_ap), writes=[key])

    P.pool(lambda e: e.memset(ONESF[:, :], 1.0), writes=["ONESF"])
    P.pool(lambda e: e.memset(ONESB[:, :], 1.0), writes=["ONESB"])
    P.pool(lambda e: e.memset(NEGH[:, :], -0.5), writes=["NEGH"])
    P.pool(lambda e: e.affine_select(out=IDF[:, :], in_=ONESF[:, :], pattern=[[1, 128]],
                                     compare_op=ALU.is_equal, fill=0.0, base=0,
                                     channel_multiplier=-1), reads=["ONESF"], writes=["IDF"])
    P.dve(lambda e: e.tensor_copy(out=IDB[:, :], in_=IDF[:, :]), reads=["IDF"], writes=["IDB"])
    PRN = ("norm_g", "branch_norm_g", "pool_b", "pool_scale", "conv_w", "conv_b", "lru_ba", "lru_bx", "lru_lambda", "sgu_norm_g")
    PRK = [("PR", n) for n in PRN]
    P.alias[("RT2", 0)] = [("RT2", 0), "ONESF"] + PRK
    P.alias[("RT2", 1)] = [("RT2", 1), "IDF"]
    P.pool(lambda e: e.memset(PR[:, :], 0.0), writes=PRK)
    P.pool(lambda e: e.memset(PW[:, :, :, :], 0.0), writes=[("PW", 0), ("PW", 1)])
    P.pool(lambda e: e.memset(WA[:, :, :, :], 0.0), writes=[("WA", 0), ("WA", 1)])
    P.pool(lambda e: e.memset(WX[:, :, :, :], 0.0), writes=[("WX", 0), ("WX", 1)])
    P.pool(lambda e: e.memset(HST[:, :, :], 0.0), writes=[("HST", l, k) for l in range(2) for k in range(4)])
    P.pool(lambda e: e.memset(BXW[:, :, :, 0:16], 0.0), writes=[("BXW", 0), ("BXW", 1)])
    P.pool(lambda e: e.memset(CXW[:, :, :, 0:4], 0.0), writes=[("CXW", 0), ("CXW", 1)])
    wins = {(0, 0): 2, (1, 0): 4, (0, 1): 8, (1, 1): 16}
    for (hh, k), win in wins.items():
        prt = slice(hh * 64, hh * 64 + 64)
        P.pool(lambda e, prt=prt, k=k, win=win: e.memset(INVC[prt, k, :], 1.0 / win), writes=["INVC"])
        for t in range(win - 1):
            P.pool(lambda e, prt=prt, k=k, t=t: e.memset(INVC[prt, k, t:t + 1], 1.0 / (t + 1)), writes=["INVC"])

    def rows(name, r0, nrows):
        src = bass.AP(dr[name], 0, [[128, nrows], [1, 128]])
        setup_dma(PR[r0:r0 + nrows, :], src, ("PR", name))
    R_BG, R_PB, R_PS, R_CW, R_CB, R_BA, R_BX, R_LAM, R_GV = 16, 32, 36, 40, 72, 80, 88, 96, 104
    R_HBA, R_HBX = 108, 116
    rows("norm_g", 0, 16)
    rows("branch_norm_g", R_BG, 16)
    rows("pool_b", R_PB, 4)
    rows("pool_scale", R_PS, 4)
    rows("conv_w", R_CW, 32)
    rows("conv_b", R_CB, 8)
    rows("lru_ba", R_BA, 8)
    rows("lru_bx", R_BX, 8)
    rows("lru_lambda", R_LAM, 8)
    rows("sgu_norm_g", R_GV, 4)
    setup_dma(GF[:, :], bass.AP(dr["final_g"], 0, [[0, 128], [1, D]]), "GF")
    for hh in range(2):
        setup_dma(SB_[hh * 64:hh * 64 + 64, :, :, :],
                  bass.AP(dr["sgu_b"], hh * 128, [[0, 64], [512, 2], [256, 2], [1, 128]]), ("SBIAS", hh))
    setup_dma(SW, bass.AP(dr["sgu_w"], 0, [[128, 128], [16384, 8], [1, 128]]), ("HN", 0), eng="pool")
    for hh in range(2):
        prt = slice(hh * 64, hh * 64 + 64)
        setup_dma(PW[prt, :, :, hh * 64:hh * 64 + 64],
                  bass.AP(dr["pool_w"], hh * 4096, [[64, 64], [16384, 2], [8192, 2], [1, 64]]), ("PW", hh), eng="pool")
        setup_dma(WA[prt, :, :, hh * 64:hh * 64 + 64],
                  bass.AP(dr["lru_wa"], hh * 4096, [[64, 64], [32768, 2], [8192, 4], [1, 64]]), ("WA", hh), eng="pool")
        setup_dma(WX[prt, :, :, hh * 64:hh * 64 + 64],
                  bass.AP(dr["lru_wx"], hh * 4096, [[64, 64], [32768, 2], [8192, 4], [1, 64]]), ("WX", hh), eng="pool")

    def wkey(kind, l, cb):
        return ("W", kind, l, cb)

    def load_w_piece(kind, l, cb, eng):
        g = "w_%s_%d_%d" % (kind, l, cb)
        P.dma_group(g, "seq")
        if kind == "in":
            src = bass.AP(dr["w_in"], l * D * DIN + cb * 256, [[DIN, 128], [128 * DIN, 8], [1, 256]])
            dst = WIN[:, l, :, cb * 256:(cb + 1) * 256]
        else:
            src = bass.AP(dr["w_out"], l * D * D + cb * 256, [[D, 128], [128 * D, 8], [1, 256]])
            dst = WOUT[:, l, :, cb * 256:(cb + 1) * 256]
        P.dma(eng, g, lambda e: e.dma_start(out=dst, in_=src), writes=[wkey(kind, l, cb)])

    order_in = [1, 0, 3, 5, 6, 2, 4, 7, 8]
    for l in range(2):
        for cb in order_in:
            load_w_piece("in", l, cb, "pool")
        for cb in range(4):
            load_w_piece("out", l, cb, "pool")

    def win_keys(l, c0, c1):
        return [wkey("in", l, cb) for cb in range(c0 // 256, (c1 - 1) // 256 + 1)]

    P.pe(lambda e: e.transpose(out=ps_o[:, 0:128], in_=PR[:, :], identity=IDF[:, :]),
         reads=PRK + ["IDF"], banks=[B_O])
    P.dve(lambda e: e.tensor_copy(out=PC[:, :], in_=ps_o[:, 0:128]), writes=["PC"], banks=[B_O])
    P.dve(lambda e: e.tensor_scalar(out=PC[:, R_HBA:R_HBA + 16], in0=PC[:, R_BA:R_BA + 16], scalar1=0.5, scalar2=None,
                                    op0=ALU.mult), reads=["PC"], writes=["PC"])
    for l in range(2):
        wk = [wkey("out", l, cb) for cb in range(4)]
        for k in range(8):
            P.dve(lambda e, l=l, k=k: e.tensor_scalar(out=WOUT[:, l, k, :], in0=WOUT[:, l, k, :],
                                                      scalar1=PC[:, R_BG + l * 8 + k:R_BG + l * 8 + k + 1], scalar2=None,
                                                      op0=ALU.mult), reads=wk + ["PC"], writes=wk, cost=700)

    for l in range(2):
        for cb in order_in:
            for dc in range(8):
                eng = P.dve
                eng(lambda e, l=l, dc=dc, cb=cb: e.tensor_scalar(out=WIN[:, l, dc, cb * 256:(cb + 1) * 256],
                                                                 in0=WIN[:, l, dc, cb * 256:(cb + 1) * 256],
                                                                 scalar1=PC[:, l * 8 + dc:l * 8 + dc + 1], scalar2=None,
                                                                 op0=ALU.mult),
                    reads=[wkey("in", l, cb), "PC"], writes=[wkey("in", l, cb)], cost=350)
    for l in range(2):
        for k in range(4):
            for tp in range(4):
                col = R_CW + l * 16 + tp * 4 + k
                P.dve(lambda e, l=l, k=k, tp=tp, col=col: e.tensor_scalar(out=DIAG[:, l, k, tp, :], in0=IDF[:, :],
                                                                          scalar1=PC[:, col:col + 1], scalar2=None,
                                                                          op0=ALU.mult),
                      reads=["IDF", "PC"], writes=["DIAG"], cost=200)
    lam = PC[:, R_LAM:R_LAM + 8]
    e_ = TMPS[:, 0, :]
    ln_ = TMPS[:, 1, :]
    t1_ = TMPS[:, 2, :]
    t2_ = TMPS[:, 3, :]
    msk = TMPS[:, 4, :]
    P.act(lambda e: e.activation(out=e_, in_=lam, func=AF.Exp, scale=-1.0), reads=["PC"], writes=["t_e"])
    P.act(lambda e: e.activation(out=ln_, in_=e_, func=AF.Ln, bias=1.0), reads=["t_e"], writes=["t_ln"])
    P.dve(lambda e: e.tensor_scalar(out=t1_, in0=e_, scalar1=-0.25, scalar2=1.0 / 3.0, op0=ALU.mult, op1=ALU.add),
          reads=["t_e"], writes=["t_1"])
    P.dve(lambda e: e.tensor_tensor(out=t2_, in0=t1_, in1=e_, op=ALU.mult), reads=["t_1", "t_e"], writes=["t_2"])
    P.dve(lambda e: e.scalar_tensor_tensor(out=t1_, in0=t2_, scalar=-0.5, in1=e_, op0=ALU.add, op1=ALU.mult),
          reads=["t_2", "t_e"], writes=["t_1"])
    P.dve(lambda e: e.scalar_tensor_tensor(out=t2_, in0=t1_, scalar=1.0, in1=e_, op0=ALU.add, op1=ALU.mult),
          reads=["t_1", "t_e"], writes=["t_2"])
    P.dve(lambda e: e.tensor_single_scalar(out=msk, in_=e_, scalar=0.05, op=ALU.is_lt), reads=["t_e"], writes=["t_m"])
    P.dve(lambda e: e.tensor_tensor(out=t2_, in0=t2_, in1=ln_, op=ALU.subtract), reads=["t_2", "t_ln"], writes=["t_2"])
    P.dve(lambda e: e.tensor_tensor(out=t2_, in0=t2_, in1=msk, op=ALU.mult), reads=["t_2", "t_m"], writes=["t_2"])
    P.dve(lambda e: e.tensor_tensor(out=t2_, in0=t2_, in1=ln_, op=ALU.add), reads=["t_2", "t_ln"], writes=["t_2"])
    P.dve(lambda e: e.tensor_scalar(out=CA[:, 0, :], in0=t2_, scalar1=-4.0, scalar2=None, op0=ALU.mult),
          reads=["t_2"], writes=["CA"])
    P.dve(lambda e: e.tensor_scalar(out=CA[:, 1, :], in0=t2_, scalar1=-8.0, scalar2=None, op0=ALU.mult),
          reads=["t_2", "CA"], writes=["CA"])
    for q in range(8):
        P.pe(lambda e, q=q: e.transpose(out=ps_tr[0][:, q * 128:(q + 1) * 128], in_=SW[:, q, :], identity=IDB[:, :]),
             reads=[("HN", 0), "IDB"], banks=[B_TR[0]])
    P.dve(lambda e: e.tensor_copy(out=WT[:, :, :], in_=ps_tr[0][:, :].rearrange("p (q i) -> p q i", q=8)),
          writes=["WT"], banks=[B_TR[0]])
    P.dve(lambda e: e.memset(WT[64:128, :, 0:64], 0.0), reads=["WT"], writes=["WT"])

    def pc(col):
        return PC[:, col:col + 1]

    def rstd_small(c0, n, width):
        keys_in = [("SS", c0 + j) for j in range(n)]
        keys_out = [("RS", c0 + j) for j in range(n)]
        P.pool(lambda e: e.tensor_scalar(out=RS[:, c0:c0 + n], in0=SS[:, c0:c0 + n], scalar1=1.0 / width, scalar2=EPS,
                                         op0=ALU.mult, op1=ALU.add), reads=keys_in, writes=keys_out)
        P.pool(lambda e: e.tensor_tensor(out=RS[:, c0:c0 + n], in0=RS[:, c0:c0 + n], in1=NEGH[:, 0:n], op=ALU.pow),
               reads=keys_out + ["NEGH"], writes=keys_out)

    def chunk_layer(c, l):
        slot = c % NXS
        xk = [[("XH", slot, s, 0), ("XH", slot, s, 1)] for s in range(NS)]
        htk = [("HT", s) for s in range(NS)]
        for s in range(NS):
            P.act(lambda e, s=s: e.activation(out=HN[:, s % NHN, :], in_=XS[:, slot, s, :], func=AF.Square,
                                              accum_out=SS[:, s:s + 1]),
                  reads=xk[s], writes=[("HN", s % NHN), ("SS", s)])
        rstd_small(0, NS, float(D))
        for s in range(NS):
            hs = s % NHN
            P.dve(lambda e, s=s, hs=hs: e.tensor_scalar(out=HN[:, hs, :], in0=XS[:, slot, s, :],
                                                        scalar1=RS[:, s:s + 1], scalar2=None, op0=ALU.mult),
                  reads=xk[s] + [("RS", s)], writes=[("HN", hs)], cost=1100)
            for dc in range(8):
                P.pe(lambda e, dc=dc, hs=hs, hb=s % 2: e.transpose(out=ps_tr[hb][:, dc * 128:(dc + 1) * 128],
                                                         in_=HN[:, hs, dc * 128:(dc + 1) * 128], identity=IDB[:, :]),
                     reads=[("HN", hs), "IDB"], banks=[B_TR[s % 2]], cost=64)
            P.act(lambda e, s=s, hb=s % 2: e.copy(out=HT[:, :, s * 128:(s + 1) * 128],
                                               in_=ps_tr[hb][:, :].rearrange("p (q i) -> p q i", q=8)),
                  writes=[("HT", s)], banks=[B_TR[s % 2]])

        def proj_mm(col0, slot_i, slots=None):
            ps, bk = (slots or PSLOT)[slot_i]
            for dc in range(8):
                P.pe(lambda e, dc=dc, ps=ps: e.matmul(ps, lhsT=WIN[:, l, dc, col0:col0 + 128],
                                                      rhs=HT[:, dc, :], start=(dc == 0), stop=(dc == 7)),
                     reads=htk + win_keys(l, col0, col0 + 128), banks=[bk])
            return ps, bk

        for s in range(NS):
            for dc in range(8):
                P.pe(lambda e, s=s, dc=dc: e.matmul(ps_av[:, s, :], lhsT=HT[:, dc, s * 128:(s + 1) * 128],
                                                    rhs=WIN[:, l, dc, C_AV:C_AV + 256], start=(dc == 0), stop=(dc == 7)),
                     reads=[("HT", s)] + win_keys(l, C_AV, C_AV + 256), banks=[B_AV])
        pu = [proj_mm(C_AU + k * 128, k, AUSLOT) for k in range(2)]
        for s in range(NS):
            P.act(lambda e, s=s: e.activation(out=VN[:, s, :], in_=ps_av[:, s, :], func=AF.Square,
                                              accum_out=SS[:, 4 + s:5 + s]),
                  writes=[("VN", s), ("SS", 4 + s)], banks=[B_AV])
        for k in range(2):
            ps, bk = pu[k]
            P.act(lambda e, k=k, ps=ps: e.copy(out=U[:, k, :], in_=ps), writes=[("U", k)], banks=[bk])
        rstd_small(4, NS, 256.0)
        for s in range(NS):
            P.dve(lambda e, s=s: e.tensor_scalar(out=VN[:, s, :], in0=ps_av[:, s, :], scalar1=RS[:, 4 + s:5 + s],
                                                 scalar2=None, op0=ALU.mult),
                  reads=[("RS", 4 + s)], writes=[("VN", s)], banks=[B_AV])
        pb_ = [proj_mm(C_BX + k * 128, k, BXSLOT) for k in range(2)]
        for k in range(2):
            ps, bk = pb_[k]
            P.act(lambda e, k=k, ps=ps: e.copy(out=BXW[:, l, k, 16:16 + T], in_=ps), writes=[("BXW", l)], banks=[bk])
        for hp in range(2):
            for hh in range(2):
                h = 2 * hp + hh
                for s in range(NS):
                    P.pe(lambda e, hp=hp, hh=hh, h=h, s=s: e.matmul(
                        ps_sg[:, hh, s * 128:(s + 1) * 128], lhsT=VN[:, s, hp * 128:(hp + 1) * 128],
                        rhs=WT[:, l * 4 + h, :], start=True, stop=True),
                        reads=[("VN", s), "WT"], banks=[B_SG], cost=64)
            for hh in range(2):
                prt = slice(hh * 64, hh * 64 + 64)
                P.dve(lambda e, hp=hp, hh=hh, prt=prt: e.scalar_tensor_tensor(
                    out=T1[prt, hp, :].rearrange("p (s i) -> p s i", s=NS),
                    in0=ps_sg[prt, hh, :].rearrange("p (s i) -> p s i", s=NS),
                    scalar=PC[prt, R_GV + l * 2 + hp:R_GV + l * 2 + hp + 1],
                    in1=_bcast_mid(SB_[prt, l, hp, :], NS), op0=ALU.mult, op1=ALU.add),
                    reads=[("SBIAS", 0), ("SBIAS", 1), "PC"], writes=[("T1", hp)], banks=[B_SG])
            P.dve(lambda e, hp=hp: e.tensor_tensor(out=U[:, hp, :], in0=T1[:, hp, :], in1=U[:, hp, :], op=ALU.mult),
                  reads=[("T1", hp), ("U", hp)], writes=[("U", hp)])
        pcx = [proj_mm(C_CX + k * 128, k, CXSLOT) for k in range(4)]
        for k in range(4):
            ps, bk = pcx[k]
            P.act(lambda e, k=k, ps=ps: e.copy(out=CXW[:, l, k, 4:4 + T], in_=ps), writes=[("CXW", l)], banks=[bk])
        P.dve(lambda e: e.tensor_tensor(out=S2[:, :, 1:W_], in0=BXW[:, l, :, 1:W_], in1=BXW[:, l, :, 0:W_ - 1], op=ALU.add),
              reads=[("BXW", l)], writes=["S2"] + K3C)
        P.dve(lambda e: e.tensor_tensor(out=S4[:, :, 3:W_], in0=S2[:, :, 3:W_], in1=S2[:, :, 1:W_ - 2], op=ALU.add),
              reads=["S2"], writes=["S4"])
        P.dve(lambda e: e.tensor_tensor(out=S8[:, 7:W_], in0=S4[:, 1, 7:W_], in1=S4[:, 1, 3:W_ - 4], op=ALU.add),
              reads=["S4"], writes=["S8"])
        P.dve(lambda e: e.tensor_tensor(out=S16[64:128, 15:W_], in0=S8[64:128, 15:W_], in1=S8[64:128, 7:W_ - 8], op=ALU.add),
              reads=["S8"], writes=["S16"])
        srcs = {(0, 0): (lambda: S2[0:64, 0, :], "S2"), (1, 0): (lambda: S4[64:128, 0, :], "S4"),
                (0, 1): (lambda: S8[0:64, :], "S8"), (1, 1): (lambda: S16[64:128, :], "S16")}
        for (hh, k), win in wins.items():
            prt = slice(hh * 64, hh * 64 + 64)
            src, skey = srcs[(hh, k)]
            P.dve(lambda e, src=src, k=k, prt=prt, win=win: e.scalar_tensor_tensor(
                out=PB[prt, k, :], in0=src()[:, 16:W_], scalar=1.0 / win, in1=BXW[prt, l, k, 16:W_],
                op0=ALU.mult, op1=ALU.subtract), reads=[skey, ("BXW", l)], writes=[("PB", k)])
            if c == 0:
                P.dve(lambda e, src=src, k=k, prt=prt: e.tensor_tensor(
                    out=YB[prt, k, 0:16], in0=src()[:, 16:32], in1=INVC[prt, k, :], op=ALU.mult),
                    reads=[skey, "INVC"], writes=[("YB", k)])
                P.dve(lambda e, k=k, prt=prt: e.tensor_tensor(
                    out=PB[prt, k, 0:16], in0=YB[prt, k, 0:16], in1=BXW[prt, l, k, 16:32], op=ALU.subtract),
                    reads=[("YB", k), ("BXW", l)], writes=[("PB", k)])
        P.pool(lambda e: e.tensor_copy(out=BXW[:, l, :, 0:16], in_=BXW[:, l, :, T:T + 16]),
               reads=[("BXW", l)], writes=[("BXW", l)])
        for k in range(2):
            P.pe(lambda e, k=k: e.matmul(ps_sm[:, k, :], lhsT=PW[:, l, k, :], rhs=PB[:, k, :], start=True, stop=True),
                 reads=[("PB", k), ("PW", 0), ("PW", 1)], banks=[B_SM])
        for k in range(2):
            P.dve(lambda e, k=k: e.tensor_scalar(out=YB[:, k, :], in0=ps_sm[:, k, :],
                                                 scalar1=pc(R_PB + l * 2 + k), scalar2=pc(R_PS + l * 2 + k),
                                                 op0=ALU.add, op1=ALU.mult),
                  reads=["PC"], writes=[("YB", k)], banks=[B_SM])

        for k in range(4):
            psc, bkc = CONVSLOT[k]
            for tp in range(4):
                P.pe(lambda e, k=k, tp=tp, psc=psc: e.matmul(psc, lhsT=DIAG[:, l, k, tp, :], rhs=CXW[:, l, k, 1 + tp:1 + tp + T],
                                                             start=(tp == 0), stop=(tp == 3)),
                     reads=[("CXW", l), "DIAG"], banks=[bkc])
            P.act(lambda e, k=k, psc=psc: e.activation(out=CONVB[:, k, :], in_=psc, func=AF.Identity,
                                                       bias=pc(R_CB + l * 4 + k)),
                  reads=["PC"], writes=[("CONVB", k)], banks=[bkc])
        zcols = [C_AZ, C_AZ + 128, C_BZ, C_BZ + 128, C_CZ, C_CZ + 128, C_CZ + 256, C_CZ + 384]
        pz = {}
        for k in range(8):
            pz[k] = proj_mm(zcols[k], k, ZSLOT)

        P.pool(lambda e: e.tensor_copy(out=CXW[:, l, :, 0:4], in_=CXW[:, l, :, T:T + 4]),
               reads=[("CXW", l)], writes=[("CXW", l)])
        for k in range(4):
            j = l * 4 + k
            gps, gbk = GB[k % 2]
            P.pe(lambda e, k=k, gps=gps: e.matmul(gps[:, 0, :], lhsT=WA[:, l, k, :], rhs=CONVB[:, k, :], start=True, stop=True),
                 reads=[("CONVB", k), ("WA", 0), ("WA", 1)], banks=[gbk])
            P.pe(lambda e, k=k, gps=gps: e.matmul(gps[:, 1, :], lhsT=WX[:, l, k, :], rhs=CONVB[:, k, :], start=True, stop=True),
                 reads=[("CONVB", k), ("WX", 0), ("WX", 1)], banks=[gbk])
            P.act(lambda e, j=j, k=k, gps=gps: e.activation(out=RT2[:, k % 2, :], in_=gps[:, 0, :], func=AF.Tanh, scale=0.5, bias=pc(R_HBA + j)),
                  reads=["PC"], writes=[("RT2", k % 2)], banks=[gbk], tset="A")
            P.act(lambda e, j=j, k=k, gps=gps: e.activation(out=TI4[:, k, :], in_=gps[:, 1, :], func=AF.Tanh, scale=0.5, bias=pc(R_HBX + j)),
                  reads=["PC"], writes=[("TI4", k)] + (K3B if k == 0 else []), banks=[gbk], tset="A")
            P.act(lambda e, j=j, k=k: e.activation(out=AA4[:, k, :], in_=RT2[:, k % 2, :], func=AF.Exp,
                                                   scale=CA[:, 0, j:j + 1], bias=CA[:, 0, j:j + 1]),
                  reads=[("RT2", k % 2), "CA"], writes=[("AA4", k)], tset="A")
            P.pool(lambda e, k=k: e.tensor_tensor(out=M4[k][:, :], in0=AA4[:, k, :], in1=AA4[:, k, :], op=ALU.mult),
                   reads=[("AA4", k)], writes=[("M4", k)])
        for k in range(8):
            ps, bk = pz[k]
            P.act(lambda e, k=k, ps=ps: e.activation(out=SZ8[:, k, :], in_=ps, func=AF.Silu),
                  writes=[("SZ8", k)], banks=[bk], tset="S")
        ysrc = [(lambda k=k: U[:, k, :], ("U", k)) for k in range(2)] + \
               [(lambda k=k: YB[:, k, :], ("YB", k)) for k in range(2)] + \
               [(lambda k=k: YC[:, k, :], ("YC", k)) for k in range(4)]
        kb = [0, 0, 1, 1, 2, 2, 2, 2]
        sqrot = [0]

        def branch_stats(b, ks, width):
            pslot = b % 2
            for j, k in enumerate(ks):
                src, skey = ysrc[k]
                q = sqrot[0] % 2
                sqrot[0] += 1
                if b < 2:
                    P.pool(lambda e, src=src, q=q: e.tensor_tensor(out=SQ[:, q, :], in0=src(), in1=src(), op=ALU.mult),
                           reads=[skey], writes=[("SQ", q)])
                else:
                    P.dve(lambda e, src=src, q=q: e.tensor_tensor(out=SQ[:, q, :], in0=src(), in1=src(), op=ALU.mult),
                          reads=[skey], writes=[("SQ", q)])
                P.pe(lambda e, q=q, j=j, n=len(ks), pslot=pslot: e.matmul(
                    ps_sm[:, pslot, :], lhsT=ONESB[:, :], rhs=SQ[:, q, :], start=(j == 0), stop=(j == n - 1)),
                    reads=[("SQ", q), "ONESB"], banks=[B_SM])
            P.act(lambda e, b=b, pslot=pslot, width=width: e.activation(
                out=RB[:, b, :], in_=ps_sm[:, pslot, :], func=AF.Ln, scale=1.0 / width, bias=EPS),
                writes=[("RB", b)], banks=[B_SM], tset="L")
            P.act(lambda e, b=b: e.activation(out=RB[:, b, :], in_=RB[:, b, :], func=AF.Exp, scale=-0.5),
                  reads=[("RB", b)], writes=[("RB", b)], tset="L")

        def finish_early(k):
            src, skey = ysrc[k]
            P.dve(lambda e, k=k, src=src: e.tensor_tensor(out=src(), in0=src(), in1=SZ8[:, k, :], op=ALU.mult),
                  reads=[skey, ("SZ8", k)], writes=[skey])

        def finish_late(k):
            src, skey = ysrc[k]
            if k % 2 == 0:
                P.dve(lambda e, k=k, src=src: e.tensor_tensor(out=YT[:, k, :], in0=src(), in1=RB[:, kb[k], :], op=ALU.mult),
                      reads=[skey, ("RB", kb[k])], writes=[("YT", k)])
            else:
                P.dve(lambda e, k=k, src=src: e.tensor_tensor(out=YT[:, k, :], in0=src(), in1=RB[:, kb[k], :], op=ALU.mult),
                      reads=[skey, ("RB", kb[k])], writes=[("YT", k)])

        def finish(k):
            finish_early(k)
            finish_late(k)

        branch_stats(0, [0, 1], 256.0)
        branch_stats(1, [2, 3], 256.0)
        for k in range(4):
            P.act(lambda e, k=k: e.activation(out=M4[k][:, :], in_=M4[k][:, :], func=AF.Ln, scale=-1.0, bias=1.0),
                  reads=[("M4", k)], writes=[("M4", k)], tset="L")
            P.act(lambda e, k=k: e.activation(out=M4[k][:, :], in_=M4[k][:, :], func=AF.Exp, scale=0.5, bias=-0.6931471805599453),
                  reads=[("M4", k)], writes=[("M4", k)], tset="L")
        for k in range(4):
            P.dve(lambda e, k=k: e.scalar_tensor_tensor(out=BT2[:, k % 2, :], in0=TI4[:, k, :], scalar=1.0, in1=CONVB[:, k, :],
                                                        op0=ALU.add, op1=ALU.mult),
                  reads=[("TI4", k), ("CONVB", k)], writes=[("BT2", k % 2)])
            P.pool(lambda e, k=k: e.tensor_tensor(out=BT2[:, k % 2, :], in0=BT2[:, k % 2, :], in1=M4[k][:, :], op=ALU.mult),
                   reads=[("BT2", k % 2), ("M4", k)], writes=[("BT2", k % 2)])
            P.dve(lambda e, k=k: e.tensor_tensor_scan(out=YC[:, k, :], data0=AA4[:, k, :], data1=BT2[:, k % 2, :],
                                                      initial=HST[:, l, k:k + 1], op0=ALU.mult, op1=ALU.add),
                  reads=[("AA4", k), ("BT2", k % 2), ("HST", l, k)], writes=[("YC", k)], cost=900)
            P.dve(lambda e, k=k: e.tensor_copy(out=HST[:, l, k:k + 1], in_=YC[:, k, T - 1:T]),
                  reads=[("YC", k)], writes=[("HST", l, k)], cost=120)
            finish(k)
        branch_stats(2, [4, 5, 6, 7], 512.0)
        for k in range(4, 8):
            finish(k)
        ytk = [("YT", k) for k in range(8)]
        for kgroup in (range(0, 4), range(4, 8)):
          for s in range(NS):
            for half in range(2):
                pso, bko = POUT[(s * 2 + half) % 4]
                for k in kgroup:
                    P.pe(lambda e, s=s, half=half, k=k, pso=pso: e.matmul(
                        pso, lhsT=YT[:, k, s * 128:(s + 1) * 128], rhs=WOUT[:, l, k, half * 512:(half + 1) * 512],
                        start=(k == 0), stop=(k == 7)),
                        reads=[("YT", k), wkey("out", l, 2 * half), wkey("out", l, 2 * half + 1)], banks=[bko], cost=225)
        for s in range(NS):
            for half in range(2):
                pso, bko = POUT[(s * 2 + half) % 4]
                P.dve(lambda e, s=s, half=half, pso=pso: e.tensor_tensor(
                    out=XS[:, slot, s, half * 512:(half + 1) * 512], in0=pso,
                    in1=XS[:, slot, s, half * 512:(half + 1) * 512], op=ALU.add),
                    reads=[("XH", slot, s, half)], writes=[("XH", slot, s, half)], banks=[bko], cost=800)

    def final_norm(c):
        slot = c % NXS
        for s in range(NS):
            xk = [("XH", slot, s, 0), ("XH", slot, s, 1)]
            junk = YT[:, 4 * s:4 * s + 4, :]
            P.act(lambda e, s=s, junk=junk: e.activation(out=junk, in_=XS[:, slot, s, :].rearrange("p (k t) -> p k t", k=4),
                                                         func=AF.Square, accum_out=SS[:, 6 + s:7 + s]),
                  reads=xk, writes=[("YT", 4 * s + kk) for kk in range(4)] + [("SS", 6 + s)], cost=1000)
        rstd_small(6, NS, float(D))
        for s in range(NS):
            xk = [("XH", slot, s, 0), ("XH", slot, s, 1)]
            P.dve(lambda e, s=s: e.scalar_tensor_tensor(out=XS[:, slot, s, :], in0=XS[:, slot, s, :],
                                                        scalar=RS[:, 6 + s:7 + s], in1=GF[:, :],
                                                        op0=ALU.mult, op1=ALU.mult),
                  reads=xk + [("RS", 6 + s), "GF"], writes=xk, cost=1300)

    for sl in range(NXS):
        P.dma_group("xld%d" % sl, "seq")
        P.dma_group("xst%d" % sl, "seq")

    def load_x(c):
        slot = c % NXS
        src = x_d.ap()[c * T:(c + 1) * T, :].rearrange("(s p) d -> p s d", p=128)
        P.dma("sp", "xld%d" % slot, lambda e: e.dma_start(out=XS[:, slot, :, :], in_=src),
              writes=[("XH", slot, s, h) for s in range(NS) for h in range(2)], lat=8000)

    def store_x(c):
        slot = c % NXS
        dst = out_d.ap()[c * T:(c + 1) * T, :].rearrange("(s p) d -> p s d", p=128)
        P.dma("sp", "xst%d" % slot, lambda e: e.dma_start(out=dst, in_=XS[:, slot, :, :]),
              reads=[("XH", slot, s, h) for s in range(NS) for h in range(2)], lat=8000)

    if ALT_ORDER and NL == 2:
        load_x(0)
        if NCH > 1:
            load_x(1)
        chunk_layer(0, 0)
        for c in range(NCH):
            if c + 2 < NCH:
                load_x(c + 2)
            if c + 1 < NCH:
                chunk_layer(c + 1, 0)
            chunk_layer(c, 1)
            if do_final:
                final_norm(c)
            store_x(c)
    else:
        load_x(0)
        for c in range(NCH):
            if c + 1 < NCH:
                load_x(c + 1)
            for l in range(NL):
                chunk_layer(c, l)
            if do_final:
                final_norm(c)
            store_x(c)

    fw = ["xst%d" % i for i in range(NXS)]
    if "tap" in P.dma_groups:
        fw.append("tap")
    if SCHEDULE:
        P.schedule()
    P.emit(final_wait_groups=fw)
    return nc, P


def _bcast_mid(ap, n):
    pat = [list(d) for d in ap.ap]
    newpat = [pat[0], [0, n]] + pat[1:]
    return bass.AP(ap.tensor, ap.offset, newpat)


_CACHE = {}


def kernel(**inputs):
    x = np.ascontiguousarray(np.asarray(inputs["x"], dtype=np.float32))
    B, S, _ = x.shape
    key = (S,)
    nc, _ = build(S)
    params = {k: np.ascontiguousarray(np.asarray(inputs[k], dtype=np.float32)) for k in PARAM_SHAPES}
    in_maps = []
    for b in range(B):
        m = {"x": x[b]}
        m.update(params)
        in_maps.append(m)
    res = run_bass_kernel_spmd(nc, in_maps, core_ids=list(range(B)))
    out = np.stack([np.asarray(r["out"]) for r in res.results], axis=0)
    return out.astype(np.float32)
```

```python
import numpy as np
import concourse.bass as bass
import concourse.mybir as mybir
from concourse.bass_utils import run_bass_kernel_spmd

F32 = mybir.dt.float32
BF16 = mybir.dt.bfloat16
AF = mybir.ActivationFunctionType
ALU = mybir.AluOpType

SAME_ENG_SYNC = True
SCHEDULE = True
EMBED_WAIT = True
EMBED_PE = True
ALT_ORDER = True
NHN = 1
Z_BANKS = 1
CX_SL = [(6, 0), (0, 0), (0, 1), (7, 0)]
CONV_SL = [(0, 1), (2, 0), (1, 0), (1, 1)]
BX_SL = [(6, 0), (1, 0)]
AU_SL = [(7, 0), (7, 1)]
Z_SL = [(2, 0), (3, 0), (2, 1), (5, 0), (4, 0), (4, 1), (3, 1), (5, 1)]
POUT_B = [3, 4, 5, 2]
GB_B = [1, 7]
SLACK = 0.0
D = 1024
DIN = 2304
EPS = 1e-6


def _caller_line():
    import sys
    f = sys._getframe(2)
    while f is not None and f.f_code.co_name in ("add", "pe", "act", "dve", "pool", "dma", "<lambda>"):
        f = f.f_back
    return f.f_lineno if f is not None else -1


class Prog:
    ENGS = ("sp", "act", "pool", "dve", "pe")

    def __init__(self, nc):
        self.nc = nc
        self.ops = []
        self.bufs = {}
        self.dma_groups = {}
        self.alias = {}

    def dma_group(self, name, kind="seq"):
        if name not in self.dma_groups:
            self.dma_groups[name] = dict(kind=kind, n=0, sem=None)
        return name

    def add(self, eng, fn, reads=(), writes=(), dma=None, banks=(), cost=None, lat=None, tset=None):
        idx = len(self.ops)
        deps = {}
        if self.alias:
            reads = [r for k in reads for r in self.alias.get(k, (k,))]
            writes = [r for k in writes for r in self.alias.get(k, (k,))]
        for bk in banks:
            b = self.bufs.setdefault(("BANK", bk), {"w": None, "r": []})
            if b["w"] is not None:
                deps[b["w"]] = "bank"
            b["w"] = idx
        for k in reads:
            b = self.bufs.setdefault(k, {"w": None, "r": []})
            if b["w"] is not None:
                deps[b["w"]] = "raw"
        for k in writes:
            b = self.bufs.setdefault(k, {"w": None, "r": []})
            if b["w"] is not None and b["w"] not in deps:
                deps[b["w"]] = "waw"
            for r in b["r"]:
                if r not in deps:
                    deps[r] = "war"
        for k in reads:
            self.bufs[k]["r"].append(idx)
        for k in writes:
            b = self.bufs[k]
            b["w"] = idx
            b["r"] = []
        deps.pop(idx, None)
        if cost is None:
            cost = {"pe": 117, "act": 480, "dve": 560, "pool": 760, "sp": 60}[eng]
        op = dict(eng=eng, fn=fn, deps=deps, dma=dma, signal=False, cnt=None, didx=None, idx=idx,
                  cost=cost, lat=(lat if lat is not None else cost), tset=tset, line=_caller_line())
        if dma is not None:
            g = self.dma_groups[self.dma_group(dma)]
            g["n"] += 1
            op["didx"] = g["n"]
        self.ops.append(op)
        return idx

    def pe(self, fn, reads=(), writes=(), banks=(), **kw):
        return self.add("pe", fn, reads, writes, banks=banks, **kw)

    def act(self, fn, reads=(), writes=(), banks=(), **kw):
        return self.add("act", fn, reads, writes, banks=banks, **kw)

    def dve(self, fn, reads=(), writes=(), banks=(), **kw):
        return self.add("dve", fn, reads, writes, banks=banks, **kw)

    def pool(self, fn, reads=(), writes=(), banks=(), **kw):
        return self.add("pool", fn, reads, writes, banks=banks, **kw)

    def dma(self, eng, group, fn, reads=(), writes=(), **kw):
        kw.setdefault("cost", 60 if eng == "sp" else 1200)
        kw.setdefault("lat", 4000)
        return self.add(eng, fn, reads, writes, dma=group, **kw)

    def schedule(self):
        import heapq
        ops = self.ops
        n = len(ops)
        succ = [[] for _ in range(n)]
        indeg = [0] * n
        for o in ops:
            for d in o["deps"]:
                succ[d].append(o["idx"])
                indeg[o["idx"]] += 1
        prio = [0.0] * n
        for i in range(n - 1, -1, -1):
            m = 0.0
            for s in succ[i]:
                if prio[s] > m:
                    m = prio[s]
            prio[i] = ops[i]["lat"] + m
        fin = [0.0] * n
        rdy = [0.0] * n
        ready = {e: [] for e in self.ENGS}
        for o in ops:
            if indeg[o["idx"]] == 0:
                heapq.heappush(ready[o["eng"]], (-prio[o["idx"]], o["idx"]))
        free = {e: 0.0 for e in self.ENGS}
        cur_set = [None]
        last_on = {}
        rdy_src = {}
        order = []
        done = 0
        while done < n:
            best = None
            allc = []
            for e in self.ENGS:
                h = ready[e]
                if not h:
                    continue
                cands = heapq.nsmallest(6, h)
                for (np_, i) in cands:
                    st = max(free[e], rdy[i])
                    pen = 0.0
                    if e == "act" and ops[i]["tset"] is not None and cur_set[0] is not None and ops[i]["tset"] != cur_set[0]:
                        pen = 1300.0
                    allc.append((st + pen, np_, i, e, st, pen))
            mn = min(c[0] for c in allc)
            near = [c for c in allc if c[0] <= mn + SLACK]
            c = min(near, key=lambda c: (c[1], c[0], c[2]))
            key, e, i, st, pen = c[0], c[3], c[2], c[4], c[5]
            ready[e].remove((-prio[i], i))
            heapq.heapify(ready[e])
            o = ops[i]
            if e == "act" and o["tset"] is not None:
                cur_set[0] = o["tset"]
            st = st + pen
            o["st"] = st
            o["why"] = ("eng", last_on.get(e)) if free[e] >= rdy[i] else ("dep", rdy_src.get(i))
            last_on[e] = i
            free[e] = st + o["cost"]
            fin[i] = st + o["lat"] + 60.0
            order.append((st, i))
            done += 1
            for s in succ[i]:
                if fin[i] > rdy[s]:
                    rdy[s] = fin[i]
                    rdy_src[s] = i
                indeg[s] -= 1
                if indeg[s] == 0:
                    heapq.heappush(ready[ops[s]["eng"]], (-prio[s], s))
        order.sort()
        self.order = [i for _, i in order]
        self.sim_time = max(fin)

    def _need_sync(self, x, y, kind):
        if y["dma"] is not None or x["dma"] is not None:
            return True
        if x["eng"] != y["eng"]:
            return True
        if x["eng"] == "pe" or kind == "bank":
            return False
        return SAME_ENG_SYNC

    def emit(self, final_wait_groups=()):
        nc = self.nc
        ops = self.ops
        order = getattr(self, "order", None) or list(range(len(ops)))
        seq = [ops[i] for i in order]
        for x in ops:
            for yi, kind in x["deps"].items():
                y = ops[yi]
                if y["dma"] is None and self._need_sync(x, y, kind):
                    y["signal"] = True
        cnt = {e: 0 for e in self.ENGS}
        gcnt = {g: 0 for g in self.dma_groups}
        for o in seq:
            if o["dma"] is not None:
                gcnt[o["dma"]] += 1
                o["didx"] = gcnt[o["dma"]]
            if o["dma"] is None and o["signal"]:
                cnt[o["eng"]] += 1
                o["cnt"] = cnt[o["eng"]]
        esem = {e: nc.alloc_semaphore("sem_" + e) for e in self.ENGS}
        for name, g in self.dma_groups.items():
            g["sem"] = nc.alloc_semaphore("dsem_" + name)
        self.stats = dict(cnt=dict(cnt), nops={e: 0 for e in self.ENGS},
                          nwaits={e: 0 for e in self.ENGS})

        def wait_target(y):
            if y["dma"] is not None:
                g = self.dma_groups[y["dma"]]
                if g["kind"] == "all":
                    return g["sem"], 16 * g["n"]
                return g["sem"], 16 * y["didx"]
            return esem[y["eng"]], y["cnt"]

        with nc.Block() as block:
            def run_engine(ename, e):
                waited = {}
                for x in seq:
                    if x["eng"] != ename:
                        continue
                    need = {}
                    for yi, kind in x["deps"].items():
                        y = ops[yi]
                        if not self._need_sync(x, y, kind):
                            continue
                        sem, val = wait_target(y)
                        key = id(sem)
                        if val > need.get(key, (None, 0))[1]:
                            need[key] = (sem, val)
                    todo = [(key, sem, val) for key, (sem, val) in need.items() if waited.get(key, 0) < val]
                    emb = None
                    if EMBED_WAIT and todo and x["dma"] is None and (ename != "pe" or EMBED_PE):
                        emb = todo.pop()
                    for key, sem, val in todo:
                        e.wait_ge(sem, val)
                        waited[key] = val
                        self.stats["nwaits"][ename] += 1
                    inst = x["fn"](e)
                    if emb is not None:
                        inst._wait_ge(emb[1], emb[2])
                        waited[emb[0]] = emb[2]
                    self.stats["nops"][ename] += 1
                    if x["dma"] is not None:
                        inst.then_inc(self.dma_groups[x["dma"]]["sem"], 16)
                    elif x["signal"]:
                        inst.then_inc(esem[ename], 1)
                if ename == "sp":
                    for gname in final_wait_groups:
                        g = self.dma_groups[gname]
                        e.wait_ge(g["sem"], 16 * g["n"])

            @block.sync
            def _(e):
                run_engine("sp", e)

            @block.scalar
            def _(e):
                run_engine("act", e)

            @block.gpsimd
            def _(e):
                run_engine("pool", e)

            @block.vector
            def _(e):
                run_engine("dve", e)

            @block.tensor
            def _(e):
                run_engine("pe", e)


PARAM_SHAPES = {
    "norm_g": [2, 1024], "w_in": [2, 1024, 2304], "sgu_norm_g": [2, 256],
    "sgu_w": [2, 4, 128, 128], "sgu_b": [2, 4, 128], "pool_w": [2, 4, 64, 64],
    "pool_b": [2, 256], "pool_scale": [2, 256], "conv_w": [2, 4, 512],
    "conv_b": [2, 512], "lru_wa": [2, 8, 64, 64], "lru_ba": [2, 512],
    "lru_wx": [2, 8, 64, 64], "lru_bx": [2, 512], "lru_lambda": [2, 512],
    "branch_norm_g": [2, 1024], "w_out": [2, 1024, 1024], "final_g": [1024],
}

C_AU, C_AV, C_AZ, C_BX, C_BZ, C_CX, C_CZ = 0, 256, 512, 768, 1024, 1280, 1792


def build(S, T=256, taps=None, NL=2, do_final=True):
    NS = T // 128
    NCH = S // T
    nc = bass.Bass("TRN2", target_bir_lowering=False)
    P = Prog(nc)
    x_d = nc.dram_tensor("x", [S, D], F32, kind="ExternalInput")
    dr = {k: nc.dram_tensor(k, shp, F32, kind="ExternalInput") for k, shp in PARAM_SHAPES.items()}
    out_d = nc.dram_tensor("out", [S, D], F32, kind="ExternalOutput")
    tap_d = {}

    def sb(name, shape, dt=F32):
        return nc.alloc_sbuf_tensor(name, shape, dt)

    WIN = sb("WIN", [128, 2, 8, DIN], BF16)
    WOUT = sb("WOUT", [128, 2, 8, D], BF16)
    NXS = 3 if ALT_ORDER else 2
    XS = sb("XS", [128, NXS, NS, D])
    DIAG = sb("DIAG", [128, 2, 4, 4, 128], BF16)
    GF = sb("GF", [128, D])
    WT = sb("WT", [128, 8, 128], BF16)
    SB_ = sb("SBIAS", [128, 2, 2, 128])
    PW = sb("PW", [128, 2, 2, 128], BF16)
    WA = sb("WA", [128, 2, 4, 128], BF16)
    WX = sb("WX", [128, 2, 4, 128], BF16)
    PC = sb("PC", [128, 128])
    CA = sb("CA", [128, 2, 8])
    TMPS = sb("TMPS", [128, 8, 8])
    NEGH = sb("NEGH", [128, 8])
    IDB = sb("IDB", [128, 128], BF16)
    ONESB = sb("ONESB", [128, 128], BF16)
    INVC = sb("INVC", [128, 2, 16])
    HN = sb("HN", [128, NHN, D], BF16)
    HT = sb("HT", [128, 8, T], BF16)
    SS = sb("SS", [128, 8])
    RS = sb("RS", [128, 8])
    U = sb("U", [128, 2, T])
    VN = sb("VN", [128, NS, 256], BF16)
    BXW = sb("BXW", [128, 2, 2, 16 + T])
    W_ = 16 + T
    ARENA = sb("ARENA", [128, 6 * W_])
    S2 = ARENA[:, 0:2 * W_].rearrange("p (k w) -> p k w", k=2)
    S4 = ARENA[:, 2 * W_:4 * W_].rearrange("p (k w) -> p k w", k=2)
    S8 = ARENA[:, 4 * W_:5 * W_]
    S16 = ARENA[:, 5 * W_:6 * W_]
    AA4 = ARENA[:, 0:4 * T].rearrange("p (k t) -> p k t", k=4)
    TI4 = ARENA[:, 4 * T:6 * T].bitcast(BF16).rearrange("p (k t) -> p k t", k=4)
    K3B = []
    K3C = []
    _bounds = sorted(set([0, 2 * W_, 4 * W_, 5 * W_, 6 * W_] + [k * T for k in range(5)] + [4 * T + k * (T // 2) for k in range(5)]))
    _regs = list(zip(_bounds[:-1], _bounds[1:]))

    def _cover(lo, hi):
        return [("AR", a) for (a, b) in _regs if a < hi and b > lo]
    P.alias["S2"] = _cover(0, 2 * W_)
    P.alias["S4"] = _cover(2 * W_, 4 * W_)
    P.alias["S8"] = _cover(4 * W_, 5 * W_)
    P.alias["S16"] = _cover(5 * W_, 6 * W_)
    for k in range(4):
        P.alias[("AA4", k)] = _cover(k * T, (k + 1) * T)
        P.alias[("TI4", k)] = _cover(4 * T + k * (T // 2), 4 * T + (k + 1) * (T // 2))
    PB = sb("PB", [128, 2, T], BF16)
    YB = sb("YB", [128, 2, T])
    CXW = sb("CXW", [128, 2, 4, 4 + T], BF16)
    CONV = sb("CONV", [128, 4, T])
    CONVB = sb("CONVB", [128, 4, T], BF16)
    M4 = [sb("M4_%d" % k, [128, T]) for k in range(4)]
    BT2 = sb("BT2", [128, 2, T])
    RT2 = sb("RT2", [128, 2, T])
    PR = RT2[:, 0, 0:128]
    ONESF = RT2[:, 0, 128:256]
    IDF = RT2[:, 1, 0:128]
    T1 = BT2
    for hp_ in range(2):
        P.alias[("T1", hp_)] = [("BT2", hp_)]
    YC = sb("YC", [128, 4, T])
    HST = sb("HST", [128, 2, 4])
    SQ = PB
    for q_ in range(2):
        P.alias[("SQ", q_)] = [("PB", q_)]
    RB = sb("RB", [128, 3, T])
    SZ8 = sb("SZ8", [128, 8, T], BF16)
    YT = sb("YT", [128, 8, T], BF16)
    SW = HN[:, 0, :].rearrange("p (q j) -> p q j", q=8)

    ps_tr = [nc.alloc_psum_tensor("ps_tr%d" % i, [128, 1024], BF16) for i in range(2)]
    ps_mm = [nc.alloc_psum_tensor("ps_mm%d" % i, [128, 512], F32) for i in range(2)]
    ps_av = nc.alloc_psum_tensor("ps_av", [128, NS, 256], F32)
    ps_sg = nc.alloc_psum_tensor("ps_sg", [128, 2, T], F32)
    ps_sm = nc.alloc_psum_tensor("ps_sm", [128, 2, T], F32)
    ps_o = nc.alloc_psum_tensor("ps_o", [128, 512], F32)
    B_TR, B_MM, B_AV, B_SG, B_SM, B_O = (0, 1), (2, 3), 4, 5, 6, 7
    PFULL = [(ps_mm[0][:, :], 2), (ps_mm[1][:, :], 3),
             (ps_tr[0][:, :].bitcast(F32), 0), (ps_tr[1][:, :].bitcast(F32), 1)]
    POUT = [(ps_mm[1][:, :], 3), (ps_av[:, :, :].rearrange("p s e -> p (s e)"), 4),
            (ps_sg[:, :, :].rearrange("p h t -> p (h t)"), 5), (ps_mm[0][:, :], 2)]
    GB = [(ps_sm, 6), (ps_o[:, :].rearrange("p (h t) -> p h t", h=2), 7)]
    PSLOT = [(PFULL[i % 4][0][:, (i // 4) * T:(i // 4 + 1) * T], PFULL[i % 4][1]) for i in range(8)]
    _zfull = [(ps_mm[0][:, :], 2), (ps_mm[1][:, :], 3),
              (ps_av[:, :, :].rearrange("p s e -> p (s e)"), 4), (ps_sg[:, :, :].rearrange("p h t -> p (h t)"), 5)]
    _bv = {0: ps_tr[0][:, :].bitcast(F32), 1: ps_tr[1][:, :].bitcast(F32), 2: ps_mm[0][:, :], 3: ps_mm[1][:, :],
           4: ps_av[:, :, :].rearrange("p s e -> p (s e)"), 5: ps_sg[:, :, :].rearrange("p h t -> p (h t)"),
           6: ps_sm[:, :, :].rearrange("p h t -> p (h t)"), 7: ps_o[:, :]}

    def _half(bh):
        return (_bv[bh[0]][:, bh[1] * T:(bh[1] + 1) * T], bh[0])
    CXSLOT = [_half(x) for x in CX_SL]
    CONVSLOT = [_half(x) for x in CONV_SL]
    BXSLOT = [_half(x) for x in BX_SL]
    AUSLOT = [_half(x) for x in AU_SL]
    POUT = [(_bv[b_], b_) for b_ in POUT_B]
    GB = [(_bv[b_].rearrange("p (h t) -> p h t", h=2), b_) for b_ in GB_B]
    ZSLOT = [_half(x) for x in Z_SL] if True else [(_zfull[i % 4][0][:, (i // 4) * T:(i // 4 + 1) * T], _zfull[i % 4][1]) for i in range(8)]

    def tap(name, src_ap, shape, reads):
        if taps is None or name not in taps:
            return
        t = nc.dram_tensor("tap_" + name, shape, F32, kind="ExternalOutput")
        tap_d[name] = t
        P.dma_group("tap", "all")
        P.dma("sp", "tap", lambda e: e.dma_start(out=t.ap(), in_=src_ap), reads=reads)


    nsetup = [0]

    def setup_dma(out_ap, in_ap, key, eng="sp"):
        g = "su%d" % nsetup[0]
        nsetup[0] += 1
        P.dma_group(g, "seq")
        P.dma(eng, g, lambda e: e.dma_start(out=out_ap, in_=in_ap), writes=[key])

    P.pool(lambda e: e.memset(ONESF[:, :], 1.0), writes=["ONESF"])
    P.pool(lambda e: e.memset(ONESB[:, :], 1.0), writes=["ONESB"])
    P.pool(lambda e: e.memset(NEGH[:, :], -0.5), writes=["NEGH"])
    P.pool(lambda e: e.affine_select(out=IDF[:, :], in_=ONESF[:, :], pattern=[[1, 128]],
                                     compare_op=ALU.is_equal, fill=0.0, base=0,
                                     channel_multiplier=-1), reads=["ONESF"], writes=["IDF"])
    P.dve(lambda e: e.tensor_copy(out=IDB[:, :], in_=IDF[:, :]), reads=["IDF"], writes=["IDB"])
    PRN = ("norm_g", "branch_norm_g", "pool_b", "pool_scale", "conv_w", "conv_b", "lru_ba", "lru_bx", "lru_lambda", "sgu_norm_g")
    PRK = [("PR", n) for n in PRN]
    P.alias[("RT2", 0)] = [("RT2", 0), "ONESF"] + PRK
    P.alias[("RT2", 1)] = [("RT2", 1), "IDF"]
    P.pool(lambda e: e.memset(PR[:, :], 0.0), writes=PRK)
    P.pool(lambda e: e.memset(PW[:, :, :, :], 0.0), writes=[("PW", 0), ("PW", 1)])
    P.pool(lambda e: e.memset(WA[:, :, :, :], 0.0), writes=[("WA", 0), ("WA", 1)])
    P.pool(lambda e: e.memset(WX[:, :, :, :], 0.0), writes=[("WX", 0), ("WX", 1)])
    P.pool(lambda e: e.memset(HST[:, :, :], 0.0), writes=[("HST", l, k) for l in range(2) for k in range(4)])
    P.pool(lambda e: e.memset(BXW[:, :, :, 0:16], 0.0), writes=[("BXW", 0), ("BXW", 1)])
    P.pool(lambda e: e.memset(CXW[:, :, :, 0:4], 0.0), writes=[("CXW", 0), ("CXW", 1)])
    wins = {(0, 0): 2, (1, 0): 4, (0, 1): 8, (1, 1): 16}
    for (hh, k), win in wins.items():
        prt = slice(hh * 64, hh * 64 + 64)
        P.pool(lambda e, prt=prt, k=k, win=win: e.memset(INVC[prt, k, :], 1.0 / win), writes=["INVC"])
        for t in range(win - 1):
            P.pool(lambda e, prt=prt, k=k, t=t: e.memset(INVC[prt, k, t:t + 1], 1.0 / (t + 1)), writes=["INVC"])

    def rows(name, r0, nrows):
        src = bass.AP(dr[name], 0, [[128, nrows], [1, 128]])
        setup_dma(PR[r0:r0 + nrows, :], src, ("PR", name))
    R_BG, R_PB, R_PS, R_CW, R_CB, R_BA, R_BX, R_LAM, R_GV = 16, 32, 36, 40, 72, 80, 88, 96, 104
    R_HBA, R_HBX = 108, 116
    rows("norm_g", 0, 16)
    rows("branch_norm_g", R_BG, 16)
    rows("pool_b", R_PB, 4)
    rows("pool_scale", R_PS, 4)
    rows("conv_w", R_CW, 32)
    rows("conv_b", R_CB, 8)
    rows("lru_ba", R_BA, 8)
    rows("lru_bx", R_BX, 8)
    rows("lru_lambda", R_LAM, 8)
    rows("sgu_norm_g", R_GV, 4)
    setup_dma(GF[:, :], bass.AP(dr["final_g"], 0, [[0, 128], [1, D]]), "GF")
    for hh in range(2):
        setup_dma(SB_[hh * 64:hh * 64 + 64, :, :, :],
                  bass.AP(dr["sgu_b"], hh * 128, [[0, 64], [512, 2], [256, 2], [1, 128]]), ("SBIAS", hh))
    setup_dma(SW, bass.AP(dr["sgu_w"], 0, [[128, 128], [16384, 8], [1, 128]]), ("HN", 0), eng="pool")
    for hh in range(2):
        prt = slice(hh * 64, hh * 64 + 64)
        setup_dma(PW[prt, :, :, hh * 64:hh * 64 + 64],
                  bass.AP(dr["pool_w"], hh * 4096, [[64, 64], [16384, 2], [8192, 2], [1, 64]]), ("PW", hh), eng="pool")
        setup_dma(WA[prt, :, :, hh * 64:hh * 64 + 64],
                  bass.AP(dr["lru_wa"], hh * 4096, [[64, 64], [32768, 2], [8192, 4], [1, 64]]), ("WA", hh), eng="pool")
        setup_dma(WX[prt, :, :, hh * 64:hh * 64 + 64],
                  bass.AP(dr["lru_wx"], hh * 4096, [[64, 64], [32768, 2], [8192, 4], [1, 64]]), ("WX", hh), eng="pool")

    def wkey(kind, l, cb):
        return ("W", kind, l, cb)

    def load_w_piece(kind, l, cb, eng):
        g = "w_%s_%d_%d" % (kind, l, cb)
        P.dma_group(g, "seq")
        if kind == "in":
            src = bass.AP(dr["w_in"], l * D * DIN + cb * 256, [[DIN, 128], [128 * DIN, 8], [1, 256]])
            dst = WIN[:, l, :, cb * 256:(cb + 1) * 256]
        else:
            src = bass.AP(dr["w_out"], l * D * D + cb * 256, [[D, 128], [128 * D, 8], [1, 256]])
            dst = WOUT[:, l, :, cb * 256:(cb + 1) * 256]
        P.dma(eng, g, lambda e: e.dma_start(out=dst, in_=src), writes=[wkey(kind, l, cb)])

    order_in = [1, 0, 3, 5, 6, 2, 4, 7, 8]
    for l in range(2):
        for cb in order_in:
            load_w_piece("in", l, cb, "pool")
        for cb in range(4):
            load_w_piece("out", l, cb, "pool")

    def win_keys(l, c0, c1):
        return [wkey("in", l, cb) for cb in range(c0 // 256, (c1 - 1) // 256 + 1)]

    P.pe(lambda e: e.transpose(out=ps_o[:, 0:128], in_=PR[:, :], identity=IDF[:, :]),
         reads=PRK + ["IDF"], banks=[B_O])
    P.dve(lambda e: e.tensor_copy(out=PC[:, :], in_=ps_o[:, 0:128]), writes=["PC"], banks=[B_O])
    P.dve(lambda e: e.tensor_scalar(out=PC[:, R_HBA:R_HBA + 16], in0=PC[:, R_BA:R_BA + 16], scalar1=0.5, scalar2=None,
                                    op0=ALU.mult), reads=["PC"], writes=["PC"])
    for l in range(2):
        wk = [wkey("out", l, cb) for cb in range(4)]
        for k in range(8):
            P.dve(lambda e, l=l, k=k: e.tensor_scalar(out=WOUT[:, l, k, :], in0=WOUT[:, l, k, :],
                                                      scalar1=PC[:, R_BG + l * 8 + k:R_BG + l * 8 + k + 1], scalar2=None,
                                                      op0=ALU.mult), reads=wk + ["PC"], writes=wk, cost=700)

    for l in range(2):
        for cb in order_in:
            for dc in range(8):
                eng = P.dve
                eng(lambda e, l=l, dc=dc, cb=cb: e.tensor_scalar(out=WIN[:, l, dc, cb * 256:(cb + 1) * 256],
                                                                 in0=WIN[:, l, dc, cb * 256:(cb + 1) * 256],
                                                                 scalar1=PC[:, l * 8 + dc:l * 8 + dc + 1], scalar2=None,
                                                                 op0=ALU.mult),
                    reads=[wkey("in", l, cb), "PC"], writes=[wkey("in", l, cb)], cost=350)
    for l in range(2):
        for k in range(4):
            for tp in range(4):
                col = R_CW + l * 16 + tp * 4 + k
                P.dve(lambda e, l=l, k=k, tp=tp, col=col: e.tensor_scalar(out=DIAG[:, l, k, tp, :], in0=IDF[:, :],
                                                                          scalar1=PC[:, col:col + 1], scalar2=None,
                                                                          op0=ALU.mult),
                      reads=["IDF", "PC"], writes=["DIAG"], cost=200)
    lam = PC[:, R_LAM:R_LAM + 8]
    e_ = TMPS[:, 0, :]
    ln_ = TMPS[:, 1, :]
    t1_ = TMPS[:, 2, :]
    t2_ = TMPS[:, 3, :]
    msk = TMPS[:, 4, :]
    P.act(lambda e: e.activation(out=e_, in_=lam, func=AF.Exp, scale=-1.0), reads=["PC"], writes=["t_e"])
    P.act(lambda e: e.activation(out=ln_, in_=e_, func=AF.Ln, bias=1.0), reads=["t_e"], writes=["t_ln"])
    P.dve(lambda e: e.tensor_scalar(out=t1_, in0=e_, scalar1=-0.25, scalar2=1.0 / 3.0, op0=ALU.mult, op1=ALU.add),
          reads=["t_e"], writes=["t_1"])
    P.dve(lambda e: e.tensor_tensor(out=t2_, in0=t1_, in1=e_, op=ALU.mult), reads=["t_1", "t_e"], writes=["t_2"])
    P.dve(lambda e: e.scalar_tensor_tensor(out=t1_, in0=t2_, scalar=-0.5, in1=e_, op0=ALU.add, op1=ALU.mult),
          reads=["t_2", "t_e"], writes=["t_1"])
    P.dve(lambda e: e.scalar_tensor_tensor(out=t2_, in0=t1_, scalar=1.0, in1=e_, op0=ALU.add, op1=ALU.mult),
          reads=["t_1", "t_e"], writes=["t_2"])
    P.dve(lambda e: e.tensor_single_scalar(out=msk, in_=e_, scalar=0.05, op=ALU.is_lt), reads=["t_e"], writes=["t_m"])
    P.dve(lambda e: e.tensor_tensor(out=t2_, in0=t2_, in1=ln_, op=ALU.subtract), reads=["t_2", "t_ln"], writes=["t_2"])
    P.dve(lambda e: e.tensor_tensor(out=t2_, in0=t2_, in1=msk, op=ALU.mult), reads=["t_2", "t_m"], writes=["t_2"])
    P.dve(lambda e: e.tensor_tensor(out=t2_, in0=t2_, in1=ln_, op=ALU.add), reads=["t_2", "t_ln"], writes=["t_2"])
    P.dve(lambda e: e.tensor_scalar(out=CA[:, 0, :], in0=t2_, scalar1=-4.0, scalar2=None, op0=ALU.mult),
          reads=["t_2"], writes=["CA"])
    P.dve(lambda e: e.tensor_scalar(out=CA[:, 1, :], in0=t2_, scalar1=-8.0, scalar2=None, op0=ALU.mult),
          reads=["t_2", "CA"], writes=["CA"])
    for q in range(8):
        P.pe(lambda e, q=q: e.transpose(out=ps_tr[0][:, q * 128:(q + 1) * 128], in_=SW[:, q, :], identity=IDB[:, :]),
             reads=[("HN", 0), "IDB"], banks=[B_TR[0]])
    P.dve(lambda e: e.tensor_copy(out=WT[:, :, :], in_=ps_tr[0][:, :].rearrange("p (q i) -> p q i", q=8)),
          writes=["WT"], banks=[B_TR[0]])
    P.dve(lambda e: e.memset(WT[64:128, :, 0:64], 0.0), reads=["WT"], writes=["WT"])

    def pc(col):
        return PC[:, col:col + 1]

    def rstd_small(c0, n, width):
        keys_in = [("SS", c0 + j) for j in range(n)]
        keys_out = [("RS", c0 + j) for j in range(n)]
        P.pool(lambda e: e.tensor_scalar(out=RS[:, c0:c0 + n], in0=SS[:, c0:c0 + n], scalar1=1.0 / width, scalar2=EPS,
                                         op0=ALU.mult, op1=ALU.add), reads=keys_in, writes=keys_out)
        P.pool(lambda e: e.tensor_tensor(out=RS[:, c0:c0 + n], in0=RS[:, c0:c0 + n], in1=NEGH[:, 0:n], op=ALU.pow),
               reads=keys_out + ["NEGH"], writes=keys_out)

    def chunk_layer(c, l):
        slot = c % NXS
        xk = [[("XH", slot, s, 0), ("XH", slot, s, 1)] for s in range(NS)]
        htk = [("HT", s) for s in range(NS)]
        for s in range(NS):
            P.act(lambda e, s=s: e.activation(out=HN[:, s % NHN, :], in_=XS[:, slot, s, :], func=AF.Square,
                                              accum_out=SS[:, s:s + 1]),
                  reads=xk[s], writes=[("HN", s % NHN), ("SS", s)])
        rstd_small(0, NS, float(D))
        for s in range(NS):
            hs = s % NHN
            P.dve(lambda e, s=s, hs=hs: e.tensor_scalar(out=HN[:, hs, :], in0=XS[:, slot, s, :],
                                                        scalar1=RS[:, s:s + 1], scalar2=None, op0=ALU.mult),
                  reads=xk[s] + [("RS", s)], writes=[("HN", hs)], cost=1100)
            for dc in range(8):
                P.pe(lambda e, dc=dc, hs=hs, hb=s % 2: e.transpose(out=ps_tr[hb][:, dc * 128:(dc + 1) * 128],
                                                         in_=HN[:, hs, dc * 128:(dc + 1) * 128], identity=IDB[:, :]),
                     reads=[("HN", hs), "IDB"], banks=[B_TR[s % 2]], cost=64)
            P.act(lambda e, s=s, hb=s % 2: e.copy(out=HT[:, :, s * 128:(s + 1) * 128],
                                               in_=ps_tr[hb][:, :].rearrange("p (q i) -> p q i", q=8)),
                  writes=[("HT", s)], banks=[B_TR[s % 2]])

        def proj_mm(col0, slot_i, slots=None):
            ps, bk = (slots or PSLOT)[slot_i]
            for dc in range(8):
                P.pe(lambda e, dc=dc, ps=ps: e.matmul(ps, lhsT=WIN[:, l, dc, col0:col0 + 128],
                                                      rhs=HT[:, dc, :], start=(dc == 0), stop=(dc == 7)),
                     reads=htk + win_keys(l, col0, col0 + 128), banks=[bk])
            return ps, bk

        for s in range(NS):
            for dc in range(8):
                P.pe(lambda e, s=s, dc=dc: e.matmul(ps_av[:, s, :], lhsT=HT[:, dc, s * 128:(s + 1) * 128],
                                                    rhs=WIN[:, l, dc, C_AV:C_AV + 256], start=(dc == 0), stop=(dc == 7)),
                     reads=[("HT", s)] + win_keys(l, C_AV, C_AV + 256), banks=[B_AV])
        pu = [proj_mm(C_AU + k * 128, k, AUSLOT) for k in range(2)]
        for s in range(NS):
            P.act(lambda e, s=s: e.activation(out=VN[:, s, :], in_=ps_av[:, s, :], func=AF.Square,
                                              accum_out=SS[:, 4 + s:5 + s]),
                  writes=[("VN", s), ("SS", 4 + s)], banks=[B_AV])
        for k in range(2):
            ps, bk = pu[k]
            P.act(lambda e, k=k, ps=ps: e.copy(out=U[:, k, :], in_=ps), writes=[("U", k)], banks=[bk])
        rstd_small(4, NS, 256.0)
        for s in range(NS):
            P.dve(lambda e, s=s: e.tensor_scalar(out=VN[:, s, :], in0=ps_av[:, s, :], scalar1=RS[:, 4 + s:5 + s],
                                                 scalar2=None, op0=ALU.mult),
                  reads=[("RS", 4 + s)], writes=[("VN", s)], banks=[B_AV])
        pb_ = [proj_mm(C_BX + k * 128, k, BXSLOT) for k in range(2)]
        for k in range(2):
            ps, bk = pb_[k]
            P.act(lambda e, k=k, ps=ps: e.copy(out=BXW[:, l, k, 16:16 + T], in_=ps), writes=[("BXW", l)], banks=[bk])
        for hp in range(2):
            for hh in range(2):
                h = 2 * hp + hh
                for s in range(NS):
                    P.pe(lambda e, hp=hp, hh=hh, h=h, s=s: e.matmul(
                        ps_sg[:, hh, s * 128:(s + 1) * 128], lhsT=VN[:, s, hp * 128:(hp + 1) * 128],
                        rhs=WT[:, l * 4 + h, :], start=True, stop=True),
                        reads=[("VN", s), "WT"], banks=[B_SG], cost=64)
            for hh in range(2):
                prt = slice(hh * 64, hh * 64 + 64)
                P.dve(lambda e, hp=hp, hh=hh, prt=prt: e.scalar_tensor_tensor(
                    out=T1[prt, hp, :].rearrange("p (s i) -> p s i", s=NS),
                    in0=ps_sg[prt, hh, :].rearrange("p (s i) -> p s i", s=NS),
                    scalar=PC[prt, R_GV + l * 2 + hp:R_GV + l * 2 + hp + 1],
                    in1=_bcast_mid(SB_[prt, l, hp, :], NS), op0=ALU.mult, op1=ALU.add),
                    reads=[("SBIAS", 0), ("SBIAS", 1), "PC"], writes=[("T1", hp)], banks=[B_SG])
            P.dve(lambda e, hp=hp: e.tensor_tensor(out=U[:, hp, :], in0=T1[:, hp, :], in1=U[:, hp, :], op=ALU.mult),
                  reads=[("T1", hp), ("U", hp)], writes=[("U", hp)])
        pcx = [proj_mm(C_CX + k * 128, k, CXSLOT) for k in range(4)]
        for k in range(4):
            ps, bk = pcx[k]
            P.act(lambda e, k=k, ps=ps: e.copy(out=CXW[:, l, k, 4:4 + T], in_=ps), writes=[("CXW", l)], banks=[bk])
        P.dve(lambda e: e.tensor_tensor(out=S2[:, :, 1:W_], in0=BXW[:, l, :, 1:W_], in1=BXW[:, l, :, 0:W_ - 1], op=ALU.add),
              reads=[("BXW", l)], writes=["S2"] + K3C)
        P.dve(lambda e: e.tensor_tensor(out=S4[:, :, 3:W_], in0=S2[:, :, 3:W_], in1=S2[:, :, 1:W_ - 2], op=ALU.add),
              reads=["S2"], writes=["S4"])
        P.dve(lambda e: e.tensor_tensor(out=S8[:, 7:W_], in0=S4[:, 1, 7:W_], in1=S4[:, 1, 3:W_ - 4], op=ALU.add),
              reads=["S4"], writes=["S8"])
        P.dve(lambda e: e.tensor_tensor(out=S16[64:128, 15:W_], in0=S8[64:128, 15:W_], in1=S8[64:128, 7:W_ - 8], op=ALU.add),
              reads=["S8"], writes=["S16"])
        srcs = {(0, 0): (lambda: S2[0:64, 0, :], "S2"), (1, 0): (lambda: S4[64:128, 0, :], "S4"),
                (0, 1): (lambda: S8[0:64, :], "S8"), (1, 1): (lambda: S16[64:128, :], "S16")}
        for (hh, k), win in wins.items():
            prt = slice(hh * 64, hh * 64 + 64)
            src, skey = srcs[(hh, k)]
            P.dve(lambda e, src=src, k=k, prt=prt, win=win: e.scalar_tensor_tensor(
                out=PB[prt, k, :], in0=src()[:, 16:W_], scalar=1.0 / win, in1=BXW[prt, l, k, 16:W_],
                op0=ALU.mult, op1=ALU.subtract), reads=[skey, ("BXW", l)], writes=[("PB", k)])
            if c == 0:
                P.dve(lambda e, src=src, k=k, prt=prt: e.tensor_tensor(
                    out=YB[prt, k, 0:16], in0=src()[:, 16:32], in1=INVC[prt, k, :], op=ALU.mult),
                    reads=[skey, "INVC"], writes=[("YB", k)])
                P.dve(lambda e, k=k, prt=prt: e.tensor_tensor(
                    out=PB[prt, k, 0:16], in0=YB[prt, k, 0:16], in1=BXW[prt, l, k, 16:32], op=ALU.subtract),
                    reads=[("YB", k), ("BXW", l)], writes=[("PB", k)])
        P.pool(lambda e: e.tensor_copy(out=BXW[:, l, :, 0:16], in_=BXW[:, l, :, T:T + 16]),
               reads=[("BXW", l)], writes=[("BXW", l)])
        for k in range(2):
            P.pe(lambda e, k=k: e.matmul(ps_sm[:, k, :], lhsT=PW[:, l, k, :], rhs=PB[:, k, :], start=True, stop=True),
                 reads=[("PB", k), ("PW", 0), ("PW", 1)], banks=[B_SM])
        for k in range(2):
            P.dve(lambda e, k=k: e.tensor_scalar(out=YB[:, k, :], in0=ps_sm[:, k, :],
                                                 scalar1=pc(R_PB + l * 2 + k), scalar2=pc(R_PS + l * 2 + k),
                                                 op0=ALU.add, op1=ALU.mult),
                  reads=["PC"], writes=[("YB", k)], banks=[B_SM])

        for k in range(4):
            psc, bkc = CONVSLOT[k]
            for tp in range(4):
                P.pe(lambda e, k=k, tp=tp, psc=psc: e.matmul(psc, lhsT=DIAG[:, l, k, tp, :], rhs=CXW[:, l, k, 1 + tp:1 + tp + T],
                                                             start=(tp == 0), stop=(tp == 3)),
                     reads=[("CXW", l), "DIAG"], banks=[bkc])
            P.act(lambda e, k=k, psc=psc: e.activation(out=CONVB[:, k, :], in_=psc, func=AF.Identity,
                                                       bias=pc(R_CB + l * 4 + k)),
                  reads=["PC"], writes=[("CONVB", k)], banks=[bkc])
        zcols = [C_AZ, C_AZ + 128, C_BZ, C_BZ + 128, C_CZ, C_CZ + 128, C_CZ + 256, C_CZ + 384]
        pz = {}
        for k in range(8):
            pz[k] = proj_mm(zcols[k], k, ZSLOT)

        P.pool(lambda e: e.tensor_copy(out=CXW[:, l, :, 0:4], in_=CXW[:, l, :, T:T + 4]),
               reads=[("CXW", l)], writes=[("CXW", l)])
        for k in range(4):
            j = l * 4 + k
            gps, gbk = GB[k % 2]
            P.pe(lambda e, k=k, gps=gps: e.matmul(gps[:, 0, :], lhsT=WA[:, l, k, :], rhs=CONVB[:, k, :], start=True, stop=True),
                 reads=[("CONVB", k), ("WA", 0), ("WA", 1)], banks=[gbk])
            P.pe(lambda e, k=k, gps=gps: e.matmul(gps[:, 1, :], lhsT=WX[:, l, k, :], rhs=CONVB[:, k, :], start=True, stop=True),
                 reads=[("CONVB", k), ("WX", 0), ("WX", 1)], banks=[gbk])
            P.act(lambda e, j=j, k=k, gps=gps: e.activation(out=RT2[:, k % 2, :], in_=gps[:, 0, :], func=AF.Tanh, scale=0.5, bias=pc(R_HBA + j)),
                  reads=["PC"], writes=[("RT2", k % 2)], banks=[gbk], tset="A")
            P.act(lambda e, j=j, k=k, gps=gps: e.activation(out=TI4[:, k, :], in_=gps[:, 1, :], func=AF.Tanh, scale=0.5, bias=pc(R_HBX + j)),
                  reads=["PC"], writes=[("TI4", k)] + (K3B if k == 0 else []), banks=[gbk], tset="A")
            P.act(lambda e, j=j, k=k: e.activation(out=AA4[:, k, :], in_=RT2[:, k % 2, :], func=AF.Exp,
                                                   scale=CA[:, 0, j:j + 1], bias=CA[:, 0, j:j + 1]),
                  reads=[("RT2", k % 2), "CA"], writes=[("AA4", k)], tset="A")
            P.pool(lambda e, k=k: e.tensor_tensor(out=M4[k][:, :], in0=AA4[:, k, :], in1=AA4[:, k, :], op=ALU.mult),
                   reads=[("AA4", k)], writes=[("M4", k)])
        for k in range(8):
            ps, bk = pz[k]
            P.act(lambda e, k=k, ps=ps: e.activation(out=SZ8[:, k, :], in_=ps, func=AF.Silu),
                  writes=[("SZ8", k)], banks=[bk], tset="S")
        ysrc = [(lambda k=k: U[:, k, :], ("U", k)) for k in range(2)] + \
               [(lambda k=k: YB[:, k, :], ("YB", k)) for k in range(2)] + \
               [(lambda k=k: YC[:, k, :], ("YC", k)) for k in range(4)]
        kb = [0, 0, 1, 1, 2, 2, 2, 2]
        sqrot = [0]

        def branch_stats(b, ks, width):
            pslot = b % 2
            for j, k in enumerate(ks):
                src, skey = ysrc[k]
                q = sqrot[0] % 2
                sqrot[0] += 1
                if b < 2:
                    P.dve(lambda e, src=src, q=q: e.tensor_tensor(out=SQ[:, q, :], in0=src(), in1=src(), op=ALU.mult),
                          reads=[skey], writes=[("SQ", q)])
                else:
                    P.dve(lambda e, src=src, q=q: e.tensor_tensor(out=SQ[:, q, :], in0=src(), in1=src(), op=ALU.mult),
                          reads=[skey], writes=[("SQ", q)])
                P.pe(lambda e, q=q, j=j, n=len(ks), pslot=pslot: e.matmul(
                    ps_sm[:, pslot, :], lhsT=ONESB[:, :], rhs=SQ[:, q, :], start=(j == 0), stop=(j == n - 1)),
                    reads=[("SQ", q), "ONESB"], banks=[B_SM])
            P.act(lambda e, b=b, pslot=pslot, width=width: e.activation(
                out=RB[:, b, :], in_=ps_sm[:, pslot, :], func=AF.Ln, scale=1.0 / width, bias=EPS),
                writes=[("RB", b)], banks=[B_SM], tset="L")
            P.act(lambda e, b=b: e.activation(out=RB[:, b, :], in_=RB[:, b, :], func=AF.Exp, scale=-0.5),
                  reads=[("RB", b)], writes=[("RB", b)], tset="L")

        def finish_early(k):
            src, skey = ysrc[k]
            P.dve(lambda e, k=k, src=src: e.tensor_tensor(out=src(), in0=src(), in1=SZ8[:, k, :], op=ALU.mult),
                  reads=[skey, ("SZ8", k)], writes=[skey])

        def finish_late(k):
            src, skey = ysrc[k]
            if k % 2 == 0:
                P.dve(lambda e, k=k, src=src: e.tensor_tensor(out=YT[:, k, :], in0=src(), in1=RB[:, kb[k], :], op=ALU.mult),
                      reads=[skey, ("RB", kb[k])], writes=[("YT", k)])
            else:
                P.dve(lambda e, k=k, src=src: e.tensor_tensor(out=YT[:, k, :], in0=src(), in1=RB[:, kb[k], :], op=ALU.mult),
                      reads=[skey, ("RB", kb[k])], writes=[("YT", k)])

        def finish(k):
            finish_early(k)
            finish_late(k)

        branch_stats(0, [0, 1], 256.0)
        branch_stats(1, [2, 3], 256.0)
        for k in range(4):
            P.act(lambda e, k=k: e.activation(out=M4[k][:, :], in_=M4[k][:, :], func=AF.Ln, scale=-1.0, bias=1.0),
                  reads=[("M4", k)], writes=[("M4", k)], tset="L")
            P.act(lambda e, k=k: e.activation(out=M4[k][:, :], in_=M4[k][:, :], func=AF.Exp, scale=0.5, bias=-0.6931471805599453),
                  reads=[("M4", k)], writes=[("M4", k)], tset="L")
        for k in range(4):
            P.dve(lambda e, k=k: e.scalar_tensor_tensor(out=BT2[:, k % 2, :], in0=TI4[:, k, :], scalar=1.0, in1=CONVB[:, k, :],
                                                        op0=ALU.add, op1=ALU.mult),
                  reads=[("TI4", k), ("CONVB", k)], writes=[("BT2", k % 2)])
            P.pool(lambda e, k=k: e.tensor_tensor(out=BT2[:, k % 2, :], in0=BT2[:, k % 2, :], in1=M4[k][:, :], op=ALU.mult),
                   reads=[("BT2", k % 2), ("M4", k)], writes=[("BT2", k % 2)])
            P.dve(lambda e, k=k: e.tensor_tensor_scan(out=YC[:, k, :], data0=AA4[:, k, :], data1=BT2[:, k % 2, :],
                                                      initial=HST[:, l, k:k + 1], op0=ALU.mult, op1=ALU.add),
                  reads=[("AA4", k), ("BT2", k % 2), ("HST", l, k)], writes=[("YC", k)], cost=900)
            P.dve(lambda e, k=k: e.tensor_copy(out=HST[:, l, k:k + 1], in_=YC[:, k, T - 1:T]),
                  reads=[("YC", k)], writes=[("HST", l, k)], cost=120)
            finish(k)
        branch_stats(2, [4, 5, 6, 7], 512.0)
        for k in range(4, 8):
            finish(k)
        ytk = [("YT", k) for k in range(8)]
        for kgroup in (range(0, 4), range(4, 8)):
          for s in range(NS):
            for half in range(2):
                pso, bko = POUT[(s * 2 + half) % 4]
                for k in kgroup:
                    P.pe(lambda e, s=s, half=half, k=k, pso=pso: e.matmul(
                        pso, lhsT=YT[:, k, s * 128:(s + 1) * 128], rhs=WOUT[:, l, k, half * 512:(half + 1) * 512],
                        start=(k == 0), stop=(k == 7)),
                        reads=[("YT", k), wkey("out", l, 2 * half), wkey("out", l, 2 * half + 1)], banks=[bko], cost=225)
        for s in range(NS):
            for half in range(2):
                pso, bko = POUT[(s * 2 + half) % 4]
                P.dve(lambda e, s=s, half=half, pso=pso: e.tensor_tensor(
                    out=XS[:, slot, s, half * 512:(half + 1) * 512], in0=pso,
                    in1=XS[:, slot, s, half * 512:(half + 1) * 512], op=ALU.add),
                    reads=[("XH", slot, s, half)], writes=[("XH", slot, s, half)], banks=[bko], cost=800)

    def final_norm(c):
        slot = c % NXS
        for s in range(NS):
            xk = [("XH", slot, s, 0), ("XH", slot, s, 1)]
            junk = YT[:, 4 * s:4 * s + 4, :]
            P.act(lambda e, s=s, junk=junk: e.activation(out=junk, in_=XS[:, slot, s, :].rearrange("p (k t) -> p k t", k=4),
                                                         func=AF.Square, accum_out=SS[:, 6 + s:7 + s]),
                  reads=xk, writes=[("YT", 4 * s + kk) for kk in range(4)] + [("SS", 6 + s)], cost=1000)
        rstd_small(6, NS, float(D))
        for s in range(NS):
            xk = [("XH", slot, s, 0), ("XH", slot, s, 1)]
            P.dve(lambda e, s=s: e.scalar_tensor_tensor(out=XS[:, slot, s, :], in0=XS[:, slot, s, :],
                                                        scalar=RS[:, 6 + s:7 + s], in1=GF[:, :],
                                                        op0=ALU.mult, op1=ALU.mult),
                  reads=xk + [("RS", 6 + s), "GF"], writes=xk, cost=1300)

    for sl in range(NXS):
        P.dma_group("xld%d" % sl, "seq")
        P.dma_group("xst%d" % sl, "seq")

    def load_x(c):
        slot = c % NXS
        src = x_d.ap()[c * T:(c + 1) * T, :].rearrange("(s p) d -> p s d", p=128)
        P.dma("sp", "xld%d" % slot, lambda e: e.dma_start(out=XS[:, slot, :, :], in_=src),
              writes=[("XH", slot, s, h) for s in range(NS) for h in range(2)], lat=8000)

    def store_x(c):
        slot = c % NXS
        dst = out_d.ap()[c * T:(c + 1) * T, :].rearrange("(s p) d -> p s d", p=128)
        P.dma("sp", "xst%d" % slot, lambda e: e.dma_start(out=dst, in_=XS[:, slot, :, :]),
              reads=[("XH", slot, s, h) for s in range(NS) for h in range(2)], lat=8000)

    if ALT_ORDER and NL == 2:
        load_x(0)
        if NCH > 1:
            load_x(1)
        chunk_layer(0, 0)
        for c in range(NCH):
            if c + 2 < NCH:
                load_x(c + 2)
            if c + 1 < NCH:
                chunk_layer(c + 1, 0)
            chunk_layer(c, 1)
            if do_final:
                final_norm(c)
            store_x(c)
    else:
        load_x(0)
        for c in range(NCH):
            if c + 1 < NCH:
                load_x(c + 1)
            for l in range(NL):
                chunk_layer(c, l)
            if do_final:
                final_norm(c)
            store_x(c)

    fw = ["xst%d" % i for i in range(NXS)]
    if "tap" in P.dma_groups:
        fw.append("tap")
    if SCHEDULE:
        P.schedule()
    P.emit(final_wait_groups=fw)
    return nc, P


def _bcast_mid(ap, n):
    pat = [list(d) for d in ap.ap]
    newpat = [pat[0], [0, n]] + pat[1:]
    return bass.AP(ap.tensor, ap.offset, newpat)


_CACHE = {}


def kernel(**inputs):
    x = np.ascontiguousarray(np.asarray(inputs["x"], dtype=np.float32))
    B, S, _ = x.shape
    key = (S,)
    nc, _ = build(S)
    params = {k: np.ascontiguousarray(np.asarray(inputs[k], dtype=np.float32)) for k in PARAM_SHAPES}
    in_maps = []
    for b in range(B):
        m = {"x": x[b]}
        m.update(params)
        in_maps.append(m)
    res = run_bass_kernel_spmd(nc, in_maps, core_ids=list(range(B)))
    out = np.stack([np.asarray(r["out"]) for r in res.results], axis=0)
    return out.astype(np.float32)
```
